# Optimizing a Trainium2 kernel written in Bass

```python
import math
import jax
import jax.numpy as jnp
from jax import lax
import numpy as np

D_MODEL = 1024
BATCH = 16
SEQ = 2048
DEPTH = 1

CTX_LEN = 256
GRID_W = 64
N_HEADS = 8
HEAD_DIM = 64
V_DIM = 2 * HEAD_DIM
QK_WIDTH = N_HEADS * 2 * HEAD_DIM
ATTN_WIDTH = N_HEADS * V_DIM
POOL_WINDOWS = (2, 4, 8, 16)
POOL_GROUPS = len(POOL_WINDOWS)
POOL_GROUP_DIM = 128
POOL_WIDTH = POOL_GROUPS * POOL_GROUP_DIM
N_BRANCH = 2
D_FF = 2816
ROPE_BASE = 10000.0
ROPE_AXIS_DIM = HEAD_DIM // 2
Q_BLOCK = 128
N_MOD = 9
EPS = 1e-6

Q_OFF = 0
K_OFF = Q_OFF + QK_WIDTH
V_OFF = K_OFF + QK_WIDTH
P_OFF = V_OFF + ATTN_WIDTH
G_OFF = P_OFF + POOL_WIDTH
IN_COLS = G_OFF + N_BRANCH * D_MODEL

kernel_name = "hybrid_diffattn_pool_macaron_dit"


def rms_norm(x, g):
    xf = x.astype(jnp.float32)
    y = xf * lax.rsqrt(jnp.mean(xf * xf, axis=-1, keepdims=True) + EPS)
    return (y * g.astype(jnp.float32)).astype(x.dtype)


def modulate(h, shift, scale):
    return h * (1 + scale) + shift


def swiglu(h, w_gu, w_down):
    a, b = jnp.split(h @ w_gu, 2, axis=-1)
    return (jax.nn.silu(a) * b) @ w_down


def rope_1d(x, pos):
    half = ROPE_AXIS_DIM // 2
    freqs = ROPE_BASE ** (-jnp.arange(half, dtype=jnp.float32) / half)
    ang = pos.astype(jnp.float32)[:, None] * freqs[None, :]
    cos = jnp.cos(ang)[None, :, None, None, :].astype(x.dtype)
    sin = jnp.sin(ang)[None, :, None, None, :].astype(x.dtype)
    x1, x2 = x[..., :half], x[..., half:]
    return jnp.concatenate([x1 * cos - x2 * sin, x2 * cos + x1 * sin], axis=-1)


def rope_2d(x, row_ids, col_ids):
    return jnp.concatenate([rope_1d(x[..., :ROPE_AXIS_DIM], row_ids),
                            rope_1d(x[..., ROPE_AXIS_DIM:], col_ids)], axis=-1)


def split_qk(z):
    return z.reshape(z.shape[0], z.shape[1], N_HEADS, 2, HEAD_DIM)


def split_v(z):
    return z.reshape(z.shape[0], z.shape[1], N_HEADS, V_DIM)


def diff_attn_block(q, k, v, lam):
    s = jnp.einsum("bqhcd,bkhcd->bhcqk", q, k).astype(jnp.float32) * (HEAD_DIM ** -0.5)
    p = jax.nn.softmax(s, axis=-1)
    p = p[:, :, 0] - lam * p[:, :, 1]
    return jnp.einsum("bhqk,bkhe->bqhe", p.astype(v.dtype), v)


def diff_attn_post(o, g_subln, lam_init):
    b, l = o.shape[0], o.shape[1]
    return (rms_norm(o, g_subln) * (1.0 - lam_init)).reshape(b, l, ATTN_WIDTH)


def pool_mix(u, w_pool, pool_scale):
    b, l, _ = u.shape
    t = jnp.arange(l)
    outs = []
    for g, w in enumerate(POOL_WINDOWS):
        ug = u[..., g * POOL_GROUP_DIM:(g + 1) * POOL_GROUP_DIM].astype(jnp.float32)
        cs = jnp.concatenate([jnp.zeros((b, 1, POOL_GROUP_DIM), jnp.float32),
                              jnp.cumsum(ug, axis=1)], axis=1)
        lo = jnp.clip(t - w // 2, 0, l)
        hi = jnp.clip(t + w - w // 2, 0, l)
        mean = (cs[:, hi] - cs[:, lo]) / (hi - lo).astype(jnp.float32)[None, :, None]
        pooled = (mean - ug).astype(u.dtype)
        outs.append(pooled @ w_pool[g])
    return jnp.concatenate(outs, axis=-1) * pool_scale


def branch_merge(attn, pool, gates, w_branch_attn, w_branch_pool, w_out):
    g_attn, g_pool = jnp.split(gates, N_BRANCH, axis=-1)
    y = jax.nn.sigmoid(g_attn) * (attn @ w_branch_attn) + jax.nn.sigmoid(g_pool) * (pool @ w_branch_pool)
    return y @ w_out


def setup_inputs(seed: int = 0) -> dict:
    key = jax.random.key(seed)
    ks = jax.random.split(key, 21)

    def nrm(k, shape, s):
        return jax.random.normal(k, shape, jnp.float32) * s

    return {
        "x": nrm(ks[0], (BATCH, SEQ, D_MODEL), 1.0),
        "c": nrm(ks[1], (BATCH, D_MODEL), 1.0),
        "ctx": nrm(ks[2], (BATCH, CTX_LEN, D_MODEL), 1.0),
        "c_ctx": nrm(ks[3], (D_MODEL,), 1.0),
        "w_mod": nrm(ks[4], (DEPTH, D_MODEL, N_MOD * D_MODEL), 0.5 * D_MODEL ** -0.5),
        "b_mod": nrm(ks[5], (DEPTH, N_MOD * D_MODEL), 0.01),
        "g_norm": 1.0 + nrm(ks[6], (DEPTH, 3, D_MODEL), 0.02),
        "w_ffn_gu": nrm(ks[7], (DEPTH, 2, D_MODEL, 2 * D_FF), D_MODEL ** -0.5),
        "w_ffn_down": nrm(ks[8], (DEPTH, 2, D_FF, D_MODEL), D_FF ** -0.5),
        "w_in": nrm(ks[9], (DEPTH, D_MODEL, IN_COLS), D_MODEL ** -0.5),
        "lambda_q1": nrm(ks[10], (DEPTH, HEAD_DIM), 0.1),
        "lambda_k1": nrm(ks[11], (DEPTH, HEAD_DIM), 0.1),
        "lambda_q2": nrm(ks[12], (DEPTH, HEAD_DIM), 0.1),
        "lambda_k2": nrm(ks[13], (DEPTH, HEAD_DIM), 0.1),
        "g_subln": 1.0 + nrm(ks[14], (DEPTH, V_DIM), 0.02),
        "w_pool": nrm(ks[15], (DEPTH, POOL_GROUPS, POOL_GROUP_DIM, POOL_GROUP_DIM), POOL_GROUP_DIM ** -0.5),
        "pool_scale": 1.0 + nrm(ks[16], (DEPTH, POOL_WIDTH), 0.02),
        "w_branch_attn": nrm(ks[17], (DEPTH, ATTN_WIDTH, D_MODEL), ATTN_WIDTH ** -0.5),
        "w_branch_pool": nrm(ks[18], (DEPTH, POOL_WIDTH, D_MODEL), POOL_WIDTH ** -0.5),
        "w_out": nrm(ks[19], (DEPTH, D_MODEL, D_MODEL), D_MODEL ** -0.5),
        "g_final": 1.0 + nrm(ks[20], (D_MODEL,), 0.02),
    }


def reference(x, c, ctx, c_ctx, w_mod, b_mod, g_norm, w_ffn_gu, w_ffn_down, w_in,
              lambda_q1, lambda_k1, lambda_q2, lambda_k2, g_subln, w_pool, pool_scale,
              w_branch_attn, w_branch_pool, w_out, g_final):
    b, l, _ = x.shape
    rows = l // GRID_W
    row_ids = jnp.repeat(jnp.arange(rows), GRID_W)
    col_ids = jnp.tile(jnp.arange(GRID_W), rows)
    n_blocks = l // Q_BLOCK

    lat, cx = x, ctx
    for layer in range(DEPTH):
        last = layer == DEPTH - 1
        mod_lat = (jax.nn.silu(c) @ w_mod[layer] + b_mod[layer])[:, None, :]
        mod_ctx = (jax.nn.silu(c_ctx) @ w_mod[layer] + b_mod[layer])[None, None, :]
        ml = jnp.split(mod_lat, N_MOD, axis=-1)
        mc = jnp.split(mod_ctx, N_MOD, axis=-1)

        lat = lat + 0.5 * ml[2] * swiglu(modulate(rms_norm(lat, g_norm[layer, 0]), ml[0], ml[1]),
                                         w_ffn_gu[layer, 0], w_ffn_down[layer, 0])
        cx = cx + 0.5 * mc[2] * swiglu(modulate(rms_norm(cx, g_norm[layer, 0]), mc[0], mc[1]),
                                       w_ffn_gu[layer, 0], w_ffn_down[layer, 0])

        h_lat = modulate(rms_norm(lat, g_norm[layer, 1]), ml[3], ml[4])
        h_ctx = modulate(rms_norm(cx, g_norm[layer, 1]), mc[3], mc[4])
        proj_lat = h_lat @ w_in[layer]
        q_l = split_qk(proj_lat[..., Q_OFF:K_OFF])
        k_l = split_qk(proj_lat[..., K_OFF:V_OFF])
        v_l = split_v(proj_lat[..., V_OFF:P_OFF])
        u_l = proj_lat[..., P_OFF:G_OFF]
        gates_l = proj_lat[..., G_OFF:]
        if last:
            proj_ctx = h_ctx @ w_in[layer][:, K_OFF:P_OFF]
            k_c = split_qk(proj_ctx[..., :QK_WIDTH])
            v_c = split_v(proj_ctx[..., QK_WIDTH:])
        else:
            proj_ctx = h_ctx @ w_in[layer]
            q_c = split_qk(proj_ctx[..., Q_OFF:K_OFF])
            k_c = split_qk(proj_ctx[..., K_OFF:V_OFF])
            v_c = split_v(proj_ctx[..., V_OFF:P_OFF])
            u_c = proj_ctx[..., P_OFF:G_OFF]
            gates_c = proj_ctx[..., G_OFF:]

        q_l = rope_2d(q_l, row_ids, col_ids)
        k_l = rope_2d(k_l, row_ids, col_ids)
        k_all = jnp.concatenate([k_c, k_l], axis=1)
        v_all = jnp.concatenate([v_c, v_l], axis=1)

        lam_init = 0.8 - 0.6 * math.exp(-0.3 * layer)
        lam = (jnp.exp(jnp.sum(lambda_q1[layer].astype(jnp.float32) * lambda_k1[layer].astype(jnp.float32)))
               - jnp.exp(jnp.sum(lambda_q2[layer].astype(jnp.float32) * lambda_k2[layer].astype(jnp.float32)))
               + lam_init)

        q_blocks = jnp.moveaxis(q_l.reshape(b, n_blocks, Q_BLOCK, N_HEADS, 2, HEAD_DIM), 1, 0)
        o_blocks = lax.map(lambda qb: diff_attn_block(qb, k_all, v_all, lam), q_blocks)
        o_lat = jnp.moveaxis(o_blocks, 0, 1).reshape(b, l, N_HEADS, V_DIM)
        attn_lat = diff_attn_post(o_lat, g_subln[layer], lam_init)
        pool_lat = pool_mix(u_l, w_pool[layer], pool_scale[layer])
        lat = lat + ml[5] * branch_merge(attn_lat, pool_lat, gates_l, w_branch_attn[layer],
                                         w_branch_pool[layer], w_out[layer])

        if not last:
            attn_ctx = diff_attn_post(diff_attn_block(q_c, k_c, v_c, lam), g_subln[layer], lam_init)
            pool_ctx = pool_mix(u_c, w_pool[layer], pool_scale[layer])
            cx = cx + mc[5] * branch_merge(attn_ctx, pool_ctx, gates_c, w_branch_attn[layer],
                                           w_branch_pool[layer], w_out[layer])
            cx = cx + 0.5 * mc[8] * swiglu(modulate(rms_norm(cx, g_norm[layer, 2]), mc[6], mc[7]),
                                           w_ffn_gu[layer, 1], w_ffn_down[layer, 1])

        lat = lat + 0.5 * ml[8] * swiglu(modulate(rms_norm(lat, g_norm[layer, 2]), ml[6], ml[7]),
                                         w_ffn_gu[layer, 1], w_ffn_down[layer, 1])

    return rms_norm(lat, g_final)
```

```python
from contextlib import ExitStack
import math
import numpy as np
import concourse.bass as bass
import concourse.mybir as mybir
from concourse.bass_utils import run_bass_kernel_spmd

F32 = mybir.dt.float32
BF16 = mybir.dt.bfloat16
AF = mybir.ActivationFunctionType
ALU = mybir.AluOpType
AX = mybir.AxisListType

D = 1024
L = 2048
CTX = 256
NH = 8
DFF = 2816
NFC = 22
LK = CTX + L
NKC = LK // 128
K_OFF, V_OFF, P_OFF, G_OFF = 1024, 2048, 3072, 3584
EPS = 1e-6
LAM_INIT = 0.8 - 0.6 * math.exp(-0.3 * 0)
NSCR = 6
RING = 3
TT = 512
NCORES = 8
NB = 2
MERGE_EXP = True
NIMG = 56


class Op:
    __slots__ = ("eng", "fn", "r", "w", "dma", "deps", "signal", "cnt", "ninst")

    def __init__(self, eng, fn, r, w, dma):
        self.eng, self.fn, self.r, self.w, self.dma = eng, fn, r, w, dma
        self.deps = None
        self.signal = False
        self.cnt = 0
        self.ninst = 1


class Sched:
    def __init__(self):
        self.ops = []

    def op(self, eng, fn, r=(), w=(), dma=None, ninst=1):
        pr = [k for k in r if isinstance(k, tuple) and k[0] == "ps"]
        if pr:
            r = [k for k in r if k not in pr]
            w = list(w) + pr
        o = Op(eng, fn, tuple(r), tuple(w), dma)
        o.ninst = ninst
        self.ops.append(o)

    def analyse(self):
        last_w = {}
        readers = {}
        ops = self.ops
        for i, o in enumerate(ops):
            deps = set()
            for k in o.r:
                j = last_w.get(k)
                if j is not None:
                    deps.add(j)
            for k in o.w:
                j = last_w.get(k)
                if j is not None:
                    deps.add(j)
                rd = readers.get(k)
                if rd:
                    deps.update(rd.values())
            deps.discard(i)
            if o.eng == "pe":
                deps = {j for j in deps if ops[j].eng != "pe" or ops[j].dma}
            o.deps = deps
            for j in deps:
                ops[j].signal = True
            for k in o.r:
                rd = readers.setdefault(k, {})
                rd[o.dma if o.dma else o.eng] = i
            for k in o.w:
                last_w[k] = i
                readers[k] = {}
        cnt = {}
        for o in ops:
            key = ("d", o.dma) if o.dma else ("e", o.eng)
            if o.dma:
                cnt[key] = cnt.get(key, 0) + 16 * o.ninst
                o.cnt = cnt[key]
            elif o.signal:
                cnt[key] = cnt.get(key, 0) + 1
                o.cnt = cnt[key]
        self.final = cnt
        return cnt

    def emit(self, nc, engines, sems):
        ops = self.ops
        for ename, eng in engines.items():
            pass
        streams = {e: [] for e in engines}
        for i, o in enumerate(ops):
            streams[o.eng].append(i)
        self._streams = streams

    def emit_engine(self, ename, eng, sems):
        ops = self.ops
        known = {}
        for i in self._streams[ename]:
            o = ops[i]
            need = {}
            for j in o.deps:
                d = ops[j]
                key = ("d", d.dma) if d.dma else ("e", d.eng)
                if d.cnt > need.get(key, 0):
                    need[key] = d.cnt
            for key, v in need.items():
                if known.get(key, 0) >= v:
                    continue
                eng.wait_ge(sems[key], v)
                known[key] = v
            ins = o.fn(eng)
            if o.dma:
                if not isinstance(ins, (list, tuple)):
                    ins = [ins]
                assert len(ins) == o.ninst, (len(ins), o.ninst)
                for x in ins:
                    x.then_inc(sems[("d", o.dma)], 16)
            elif o.signal:
                if isinstance(ins, (list, tuple)):
                    ins = ins[-1]
                ins.then_inc(sems[("e", o.eng)], 1)
        if ename == "sp":
            for key, v in self.final.items():
                if known.get(key, 0) < v:
                    eng.wait_ge(sems[key], v)


class Rot:
    def __init__(self, n, start=0):
        self.n, self.i, self.s = n, 0, start

    def get(self):
        v = self.s + (self.i % self.n)
        self.i += 1
        return v


class WStream:
    def __init__(self, S, fill_fn, tags=None):
        self.S, self.fill_fn = S, fill_fn
        self.record = tags is None
        self.tags = [] if tags is None else tags
        self.na = 0
        self.nf = 0
        self.held = []
        if not self.record:
            for _ in range(min(RING, len(self.tags))):
                self._fill()

    def _fill(self):
        k = self.nf
        self.fill_fn(self.S, self.tags[k], k % RING)
        self.nf += 1

    def acquire(self, tag):
        k = self.na
        self.na += 1
        if self.record:
            self.tags.append(tag)
        else:
            assert self.tags[k] == tag, (k, self.tags[k], tag)
        self.held.append(k)
        return k % RING

    def release(self):
        self.held.pop(0)
        if not self.record and self.nf < len(self.tags):
            assert all(h > self.nf - RING for h in self.held) or not self.held, (self.held, self.nf)
            self._fill()


def build_nc(stage=4):
    nc = bass.Bass("TRN2", target_bir_lowering=False)

    def din(name, shape):
        return nc.dram_tensor(name, list(shape), F32, kind="ExternalInput").ap()

    x_d = din("x", [NB, L, D])
    ctx_d = din("ctx", [NB, CTX, D])
    cc_d = din("cc", [24, 128])
    vec1_d = din("vec1", [108, 128])
    wmod_d = din("w_mod", [D, 9 * D])
    wgu_d = din("w_gu", [2, D, 2 * DFF])
    wdn_d = din("w_down", [2, DFF, D])
    win_d = din("w_in", [D, 5632])
    wba_d = din("w_ba", [D, D])
    wbp_d = din("w_bp", [512, D])
    wout_d = din("w_out", [D, D])
    wpool_d = din("w_pool", [4, 128, 128])
    lam_d = din("lam_bc", [128, 256])
    gsub_d = din("gsub_bc", [128, 128])
    ropec_d = din("ropec", [128, L])
    ropes_d = din("ropes", [128, L])
    ident_d = din("ident", [128, 128])
    perm_d = din("perm", [128, 128])
    edge_d = din("edge", [128, 64])
    out_d = nc.dram_tensor("out", [NB, L, D], F32, kind="ExternalOutput").ap()
    xsp_d = nc.dram_tensor("xsp", [4, 128, 8 * TT], F32, kind="Internal").ap()
    wimg_d = nc.dram_tensor("wimg", [NIMG, 128, 4096], BF16, kind="Internal").ap()

    es = ExitStack()
    with es:
        def sb(name, shape, dt):
            return es.enter_context(nc.sbuf_tensor("sb_" + name, list(shape), dt))

        KT = sb("KT", [128, NH, LK], BF16)
        Vb = sb("Vb", [128, NKC, NH * 130], BF16)
        xT = sb("xT", [128, 2, 8, TT], F32)
        hT = sb("hT", [128, 8, TT], BF16)
        R22 = sb("R22", [128, NFC * TT], BF16)
        attnT = sb("attnT", [128, 8, TT], BF16)
        stg = sb("stg", [128, 2, D], F32)
        ring = sb("ring", [128, RING, 4096], BF16)
        rope = sb("rope", [128, 2, TT], F32)
        Oev = sb("Oev", [128, 8, 130], F32)
        scr = sb("scr", [128, NSCR, 528], F32)
        ug = sb("ug", [128, 528], F32)
        pbuf = sb("pbuf", [128, 2, TT], BF16)
        rstd = sb("rstd", [128, TT], F32)
        uhalo = sb("uhalo", [128, 4, 4, 16], F32)
        ident_f = sb("ident_f", [128, 128], F32)
        ident_b = sb("ident_b", [128, 128], BF16)
        perm_b = sb("perm_b", [128, 128], BF16)
        ones_f = sb("ones_f", [128, 128], F32)
        vecT = sb("vecT", [128, 108], F32)
        ccT = sb("ccT", [128, 24], F32)
        scT = sb("scT", [128, 24], BF16)
        modT = sb("modT", [128, 72, 3], F32)
        AA = sb("AA", [128, 3, 8, 3], F32)
        GG = sb("GG", [128, 3, 8, 3], F32)
        lsm = sb("lsm", [128, 8], F32)
        gs_bc = sb("gs_bc", [128, 128], F32)
        edge = sb("edge", [128, 4, 16], F32)
        rsO = sb("rsO", [128, 8], F32)
        ssq = sb("ssq", [128, 4, 8], F32)
        rr = sb("rr", [128, 4, 8], F32)
        ps = es.enter_context(nc.psum_tensor("ps", [128, 8, 512], F32))
        psb = ps.bitcast(BF16)

        actT = R22[:, :].rearrange("p (f t) -> p f t", t=TT)
        qT = R22[:, 0:8 * TT].rearrange("p (f t) -> p f t", t=TT)
        yT = qT
        attn_tok = R22[:, 8 * TT:16 * TT].rearrange("p (q h e) -> p q h e", q=4, h=8)
        Ebuf = R22[:, 16 * TT:20 * TT].rearrange("p (s c t) -> p s c t", s=2, c=2)
        scrb = scr.bitcast(BF16)
        poolT = rope.bitcast(BF16)[:, :, :].rearrange("p a (b t) -> p (a b) t", t=TT)
        perm_f = scr[:, 1, 0:128]
        vec1s = scr[:, 2, 0:128]
        ccs = scr[:, 3, 0:128]
        lamb = scr[:, 4, 0:256]
        modv = modT[:, :, :].rearrange("p (a k c) j -> p a k c j", a=3, k=3, c=8)

        def psk(b):
            return ("ps", b)

        img_tags = {}

        def fill_fn(S, tag, slot):
            kind = tag[0]
            rk = ("ring", slot)
            sl = ring[:, slot, :]
            if kind != "mod":
                if tag in img_tags:
                    k = img_tags[tag]
                    S.op("pool", lambda g: g.dma_start(out=sl, in_=wimg_d[k]), r=[("img", k)], w=[rk], dma=rk)
                    return
                fill_fp32(S, tag, slot)
                k = len(img_tags)
                img_tags[tag] = k
                S.op("sp", lambda e: e.dma_start(out=wimg_d[k], in_=sl), r=[rk], w=[("img", k)], dma=("wb", slot))
                return
            fill_fp32(S, tag, slot)

        def fill_fp32(S, tag, slot):
            kind = tag[0]
            rk = ("ring", slot)
            sl = ring[:, slot, :]
            if kind == "mod":
                k = tag[1]
                src = wmod_d[:, k * 512:(k + 1) * 512].rearrange("(kc p) n -> p kc n", p=128)
                S.op("pool", lambda g: g.dma_start(out=sl.rearrange("p (kc n) -> p kc n", kc=8), in_=src),
                     w=[rk], dma=rk)
            elif kind == "gu":
                _, l, i = tag
                sa = wgu_d[l, :, i * 256:(i + 1) * 256].rearrange("(kc p) n -> p kc n", p=128)
                sbb = wgu_d[l, :, DFF + i * 256:DFF + (i + 1) * 256].rearrange("(kc p) n -> p kc n", p=128)
                dv = sl.rearrange("p (kc n) -> p kc n", kc=8)
                S.op("pool", lambda g: [g.dma_start(out=dv[:, :, 0:256], in_=sa),
                                        g.dma_start(out=dv[:, :, 256:512], in_=sbb)], w=[rk], dma=rk, ninst=2)
            elif kind == "down":
                _, l, d = tag
                src = wdn_d[l, :, d * 128:(d + 1) * 128].rearrange("(fc p) n -> p fc n", p=128)
                S.op("pool", lambda g: g.dma_start(out=sl[:, 0:NFC * 128].rearrange("p (fc n) -> p fc n", fc=NFC), in_=src),
                     w=[rk], dma=rk)
            elif kind == "in":
                c0 = tag[1]
                src = win_d[:, c0:c0 + 512].rearrange("(kc p) n -> p kc n", p=128)
                S.op("pool", lambda g: g.dma_start(out=sl.rearrange("p (kc n) -> p kc n", kc=8), in_=src),
                     w=[rk], dma=rk)
            elif kind == "G":
                pr = tag[1]
                sa = win_d[:, G_OFF + pr * 256:G_OFF + (pr + 1) * 256].rearrange("(kc p) n -> p kc n", p=128)
                sbb = win_d[:, G_OFF + D + pr * 256:G_OFF + D + (pr + 1) * 256].rearrange("(kc p) n -> p kc n", p=128)
                dv = sl.rearrange("p (kc n) -> p kc n", kc=8)
                S.op("pool", lambda g: [g.dma_start(out=dv[:, :, 0:256], in_=sa),
                                        g.dma_start(out=dv[:, :, 256:512], in_=sbb)], w=[rk], dma=rk, ninst=2)
            elif kind == "B":
                pr = tag[1]
                sa = wba_d[:, pr * 256:(pr + 1) * 256].rearrange("(kc p) n -> p kc n", p=128)
                sbb = wbp_d[:, pr * 256:(pr + 1) * 256].rearrange("(kc p) n -> p kc n", p=128)
                S.op("pool", lambda g: [g.dma_start(out=sl[:, 0:2048].rearrange("p (kc n) -> p kc n", kc=8), in_=sa),
                                        g.dma_start(out=sl[:, 2048:3072].rearrange("p (kc n) -> p kc n", kc=4), in_=sbb)],
                     w=[rk], dma=rk, ninst=2)
            elif kind == "out":
                h = tag[1]
                src = wout_d[:, h * 512:(h + 1) * 512].rearrange("(kc p) n -> p kc n", p=128)
                S.op("pool", lambda g: g.dma_start(out=sl.rearrange("p (kc n) -> p kc n", kc=8), in_=src),
                     w=[rk], dma=rk)
            elif kind == "wpool":
                src = wpool_d[:, :, :].rearrange("g p n -> p g n")
                S.op("pool", lambda g: g.dma_start(out=sl[:, 0:512].rearrange("p (g n) -> p g n", g=4), in_=src),
                     w=[rk], dma=rk)
            else:
                raise ValueError(tag)

        def program(S, ws):
            pr6 = Rot(6)
            sc = Rot(NSCR)
            alt = Rot(2)

            def sl8(slot):
                return ring[:, slot, :].rearrange("p (kc n) -> p kc n", kc=8)

            S.op("sp", lambda e: [e.dma_start(out=ident_f[:, :], in_=ident_d[:, :]),
                                  e.dma_start(out=perm_f, in_=perm_d[:, :]),
                                  e.dma_start(out=vec1s[0:108, :], in_=vec1_d[:, :]),
                                  e.dma_start(out=ccs[0:24, :], in_=cc_d[:, :]),
                                  e.dma_start(out=lamb, in_=lam_d[:, :]),
                                  e.dma_start(out=gs_bc[:, :], in_=gsub_d[:, :]),
                                  e.dma_start(out=edge[:, :, :].rearrange("p g t -> p (g t)"), in_=edge_d[:, :])],
                 w=["consts", ("scr", 1), ("scr", 2), ("scr", 3), ("scr", 4)], dma="consts", ninst=7)
            S.op("dve", lambda e: e.tensor_copy(out=ident_b[:, :], in_=ident_f[:, :]), r=["consts"], w=["ident_b"])
            S.op("dve", lambda e: e.tensor_copy(out=perm_b[:, :], in_=perm_f), r=["consts", ("scr", 1)], w=["perm_b"])
            S.op("dve", lambda e: e.memset(ones_f[:, :], 1.0), w=["ones_f"])
            S.op("dve", lambda e: e.memset(Vb[:, :, :], 0.0), w=["V"])
            vones = Vb[:, :, :].rearrange("p k (h e) -> p k h e", e=130)[:, :, :, 128:129]
            S.op("dve", lambda e: e.memset(vones, 1.0), w=["V"])
            S.op("dve", lambda e: e.memset(uhalo[:, :, :, :], 0.0), w=["uhalo"])
            S.op("pe", lambda e: e.transpose(ps[:, 7, 0:108], vec1s[0:108, :], ident_f[0:108, 0:108]),
                 r=["consts", ("scr", 2)], w=[psk(7)])
            S.op("dve", lambda e: e.tensor_copy(out=vecT[:, :], in_=ps[:, 7, 0:108]), r=[psk(7)], w=["vecT"])
            S.op("pe", lambda e: e.transpose(ps[:, 7, 0:24], ccs[0:24, :], ident_f[0:24, 0:24]),
                 r=["consts", ("scr", 3)], w=[psk(7)])
            S.op("dve", lambda e: e.tensor_copy(out=ccT[:, :], in_=ps[:, 7, 0:24]), r=[psk(7)], w=["ccT"])
            S.op("act", lambda e: e.activation(out=scT[:, :], in_=ccT[:, :], func=AF.Silu), r=["ccT"], w=["scT"])
            scv = scT[:, :].rearrange("p (j k) -> p j k", j=3)

            gnv = vecT[:, 72:96].rearrange("p (a c) -> p a c", a=3)

            def mods_part(k0, k1, a0, a1):
                for k in range(k0, k1):
                    slot = ws.acquire(("mod", k))
                    w8 = sl8(slot)

                    def mm(e, k=k, w8=w8):
                        last = None
                        for nn in range(4):
                            n = k * 4 + nn
                            for kc in range(8):
                                last = e.matmul(ps[:, 6, n * 3:(n + 1) * 3], lhsT=w8[:, kc, nn * 128:(nn + 1) * 128],
                                                rhs=scv[:, :, kc], start=(kc == 0), stop=(kc == 7))
                        return last
                    S.op("pe", mm, r=[("ring", slot), "scT"], w=[psk(6)])
                    ws.release()
                n0, n1 = k0 * 4, k1 * 4
                psm = ps[:, 6, n0 * 3:n1 * 3].rearrange("p (n j) -> p n j", j=3)
                S.op("dve", lambda e: e.tensor_tensor(out=modT[:, n0:n1, :], in0=psm,
                                                      in1=vecT[:, n0:n1].unsqueeze(2).broadcast_to([128, n1 - n0, 3]), op=ALU.add),
                     r=[psk(6), "vecT"], w=["modT"])
                na = a1 - a0
                S.op("dve", lambda e: e.tensor_scalar_add(out=AA[:, a0:a1, :, :], in0=modv[:, a0:a1, 1, :, :], scalar1=1.0),
                     r=["modT"], w=["AA"])
                S.op("dve", lambda e: e.tensor_tensor(out=AA[:, a0:a1, :, :], in0=AA[:, a0:a1, :, :],
                                                      in1=gnv[:, a0:a1, :].unsqueeze(3).broadcast_to([128, na, 8, 3]), op=ALU.mult),
                     r=["AA", "vecT"], w=["AA"])
                S.op("dve", lambda e: e.tensor_scalar_mul(out=GG[:, a0:a1, :, :], in0=modv[:, a0:a1, 2, :, :], scalar1=0.5),
                     r=["modT"], w=["GG"])

            mods_part(0, 6, 0, 1)
            lv = lamb.rearrange("p (a d) -> p a d", a=4)
            S.op("dve", lambda e: e.tensor_tensor(out=scr[:, 0, 0:64], in0=lv[:, 0, :], in1=lv[:, 1, :], op=ALU.mult),
                 r=["consts", ("scr", 4)], w=[("scr", 0)])
            S.op("dve", lambda e: e.tensor_tensor(out=scr[:, 0, 64:128], in0=lv[:, 2, :], in1=lv[:, 3, :], op=ALU.mult),
                 r=["consts", ("scr", 4), ("scr", 0)], w=[("scr", 0)])
            S.op("dve", lambda e: e.tensor_reduce(out=lsm[:, 0:2], in_=scr[:, 0, 0:128].rearrange("p (a d) -> p a d", a=2),
                                                  axis=AX.X, op=ALU.add), r=[("scr", 0)], w=["lsm"])
            S.op("act", lambda e: e.activation(out=lsm[:, 2:4], in_=lsm[:, 0:2], func=AF.Exp), r=["lsm"], w=["lsm"])
            S.op("dve", lambda e: e.tensor_tensor(out=lsm[:, 4:5], in0=lsm[:, 3:4], in1=lsm[:, 2:3], op=ALU.subtract),
                 r=["lsm"], w=["lsm"])
            S.op("dve", lambda e: e.tensor_scalar_add(out=lsm[:, 5:6], in0=lsm[:, 4:5], scalar1=-LAM_INIT),
                 r=["lsm"], w=["lsm"])
            nlam = lsm[:, 5:6]
            S.op("dve", lambda e: e.tensor_scalar_mul(out=gs_bc[:, :], in0=gs_bc[:, :], scalar1=1.0 - LAM_INIT),
                 r=["consts"], w=["gs_bc"])

            def xk(xs, c):
                return ("x", xs, c)

            def stat_sq(xs, n, c, eng="act"):
                s_ = sc.get()
                if eng == "act":
                    S.op("act", lambda e: e.activation(out=scr[:, s_, 0:n], in_=xT[:, xs, c, 0:n], func=AF.Square),
                         r=[xk(xs, c)], w=[("scr", s_)])
                else:
                    S.op("dve", lambda e: e.tensor_tensor(out=scr[:, s_, 0:n], in0=xT[:, xs, c, 0:n], in1=xT[:, xs, c, 0:n],
                                                          op=ALU.mult), r=[xk(xs, c)], w=[("scr", s_)])
                return s_

            def stat_mm(n, c, s_):
                S.op("pe", lambda e: e.matmul(ps[:, 6, 0:n], lhsT=ones_f[:, :], rhs=scr[:, s_, 0:n],
                                              start=(c == 0), stop=(c == 7)),
                     r=[("scr", s_), "ones_f"], w=[psk(6)])

            def stat_fin(n):
                S.op("dve", lambda e: e.tensor_scalar(out=rstd[:, 0:n], in0=ps[:, 6, 0:n], scalar1=1.0 / D, scalar2=EPS,
                                                      op0=ALU.mult, op1=ALU.add), r=[psk(6)], w=["rstd"])
                S.op("act", lambda e: e.activation(out=rstd[:, 0:n], in_=rstd[:, 0:n], func=AF.Sqrt), r=["rstd"], w=["rstd"])
                S.op("dve", lambda e: e.reciprocal(out=rstd[:, 0:n], in_=rstd[:, 0:n]), r=["rstd"], w=["rstd"])

            def norm_stats(xs, n):
                for c in range(8):
                    s_ = stat_sq(xs, n, c, "act" if c % 2 == 0 else "dve")
                    stat_mm(n, c, s_)
                stat_fin(n)

            def norm_mod(xs, n, m, j, have_stats=False):
                if not have_stats:
                    norm_stats(xs, n)
                for c in range(8):
                    s = sc.get()
                    S.op("dve", lambda e, c=c, s=s: e.tensor_tensor(out=scr[:, s, 0:n], in0=xT[:, xs, c, 0:n],
                                                                    in1=rstd[:, 0:n], op=ALU.mult),
                         r=[xk(xs, c), "rstd"], w=[("scr", s)])
                    S.op("act", lambda e, c=c, s=s: e.activation(out=hT[:, c, 0:n], in_=scr[:, s, 0:n], func=AF.Identity,
                                                                 bias=modv[:, m, 0, c, j:j + 1], scale=AA[:, m, c, j:j + 1]),
                         r=[("scr", s), "AA", "modT"], w=[("hT", c)])

            hT_all = [("hT", c) for c in range(8)]
            ATK = [("R", f) for f in range(8, 16)]

            def ffn(xs, n, l, j, m, have_stats=False, stats_after=False):
                norm_mod(xs, n, m, j, have_stats)
                for i in range(11):
                    slot = ws.acquire(("gu", l, i))
                    w8 = sl8(slot)
                    for jj in range(2):
                        fj = 2 * i + jj
                        pa, pb = pr6.get(), pr6.get()

                        def mma(e, w8=w8, jj=jj, pa=pa):
                            last = None
                            for kc in range(8):
                                last = e.matmul(ps[:, pa, 0:n], lhsT=w8[:, kc, jj * 128:(jj + 1) * 128], rhs=hT[:, kc, 0:n],
                                                start=(kc == 0), stop=(kc == 7))
                            return last

                        def mmb(e, w8=w8, jj=jj, pb=pb):
                            last = None
                            for kc in range(8):
                                last = e.matmul(ps[:, pb, 0:n], lhsT=w8[:, kc, 256 + jj * 128:256 + (jj + 1) * 128],
                                                rhs=hT[:, kc, 0:n], start=(kc == 0), stop=(kc == 7))
                            return last
                        if i == 0 and jj == 0:
                            for kc in range(8):
                                S.op("pe", lambda e, kc=kc, w8=w8, pa=pa: e.matmul(ps[:, pa, 0:n], lhsT=w8[:, kc, 0:128], rhs=hT[:, kc, 0:n],
                                                                                   start=(kc == 0), stop=(kc == 7)),
                                     r=[("ring", slot), ("hT", kc)], w=[psk(pa)])
                        else:
                            S.op("pe", mma, r=[("ring", slot)] + hT_all, w=[psk(pa)])
                        S.op("pe", mmb, r=[("ring", slot)] + hT_all, w=[psk(pb)])
                        s = sc.get()
                        S.op("act", lambda e, s=s, pa=pa: e.activation(out=scr[:, s, 0:n], in_=ps[:, pa, 0:n], func=AF.Silu),
                             r=[psk(pa)], w=[("scr", s)])
                        S.op("dve", lambda e, s=s, pb=pb, fj=fj: e.tensor_tensor(out=actT[:, fj, 0:n], in0=scr[:, s, 0:n],
                                                                                 in1=ps[:, pb, 0:n], op=ALU.mult),
                             r=[("scr", s), psk(pb)], w=[("R", fj)])
                    ws.release()
                act_all = [("R", f) for f in range(NFC)]
                for d in range(8):
                    slot = ws.acquire(("down", l, d))
                    wv = ring[:, slot, 0:NFC * 128].rearrange("p (fc n) -> p fc n", fc=NFC)
                    po = pr6.get()

                    def mmd(e, wv=wv, po=po):
                        last = None
                        for fc in range(NFC):
                            last = e.matmul(ps[:, po, 0:n], lhsT=wv[:, fc, :], rhs=actT[:, fc, 0:n],
                                            start=(fc == 0), stop=(fc == NFC - 1))
                        return last
                    S.op("pe", mmd, r=[("ring", slot)] + act_all, w=[psk(po)])
                    if stats_after and d > 0:
                        stat_mm(n, d - 1, pend)
                    S.op("dve", lambda e, d=d, po=po: e.scalar_tensor_tensor(out=xT[:, xs, d, 0:n], in0=ps[:, po, 0:n],
                                                                            scalar=GG[:, m, d, j:j + 1], in1=xT[:, xs, d, 0:n],
                                                                            op0=ALU.mult, op1=ALU.add),
                         r=[psk(po), "GG", xk(xs, d)], w=[xk(xs, d)])
                    if stats_after:
                        pend = stat_sq(xs, n, d, "act")
                    ws.release()
                if stats_after:
                    stat_mm(n, 7, pend)
                    stat_fin(n)

            def rope_to(pk, n, dest, dkeys):
                s0, s1, s2 = sc.get(), sc.get(), sc.get()
                psw = pr6.get()
                S.op("act", lambda e: e.activation(out=scrb[:, s0, 0:n], in_=ps[:, pk, 0:n], func=AF.Copy),
                     r=[psk(pk)], w=[("scr", s0)])
                S.op("pe", lambda e: e.matmul(ps[:, psw, 0:n], lhsT=perm_b[:, :], rhs=scrb[:, s0, 0:n], start=True, stop=True),
                     r=[("scr", s0), "perm_b"], w=[psk(psw)])
                S.op("dve", lambda e: e.tensor_tensor(out=scr[:, s1, 0:n], in0=ps[:, pk, 0:n], in1=rope[:, 0, 0:n], op=ALU.mult),
                     r=[psk(pk), "rope"], w=[("scr", s1)])
                S.op("dve", lambda e: e.tensor_tensor(out=scr[:, s2, 0:n], in0=ps[:, psw, 0:n], in1=rope[:, 1, 0:n], op=ALU.mult),
                     r=[psk(psw), "rope"], w=[("scr", s2)])
                S.op("dve", lambda e: e.tensor_tensor(out=dest, in0=scr[:, s1, 0:n], in1=scr[:, s2, 0:n], op=ALU.add),
                     r=[("scr", s1), ("scr", s2)], w=dkeys)

            def proj_fm(slot, col, n, pk, split=False):
                w8 = sl8(slot)
                if split:
                    for kc in range(8):
                        S.op("pe", lambda e, kc=kc: e.matmul(ps[:, pk, 0:n], lhsT=w8[:, kc, col:col + 128], rhs=hT[:, kc, 0:n],
                                                             start=(kc == 0), stop=(kc == 7)),
                             r=[("ring", slot), ("hT", kc)], w=[psk(pk)])
                    return

                def mm(e):
                    last = None
                    for kc in range(8):
                        last = e.matmul(ps[:, pk, 0:n], lhsT=w8[:, kc, col:col + 128], rhs=hT[:, kc, 0:n],
                                        start=(kc == 0), stop=(kc == 7))
                    return last
                S.op("pe", mm, r=[("ring", slot)] + hT_all, w=[psk(pk)])

            def load_tile(src_d, b, r0, n, xs):
                for sub in range(n // 128):
                    st = alt.get()
                    S.op("sp", lambda e, st=st, sub=sub: e.dma_start(out=stg[:, st, :],
                                                                     in_=src_d[b, r0 + sub * 128:r0 + (sub + 1) * 128, :]),
                         w=[("stg", st)], dma=("stgin", st))
                    for cg in range(2):
                        pt = pr6.get()

                        def tr(e, st=st, cg=cg, pt=pt):
                            last = None
                            for cc in range(4):
                                c = cg * 4 + cc
                                last = e.transpose(ps[:, pt, cc * 128:(cc + 1) * 128], stg[:, st, c * 128:(c + 1) * 128],
                                                   ident_f[:, :])
                            return last
                        S.op("pe", tr, r=[("stg", st), "consts"], w=[psk(pt)])
                        eng = "dve" if cg == 0 else "act"

                        def ev(e, sub=sub, cg=cg, pt=pt, eng=eng):
                            o = xT[:, xs, cg * 4:(cg + 1) * 4, sub * 128:(sub + 1) * 128]
                            i = ps[:, pt, :].rearrange("p (c t) -> p c t", c=4)
                            if eng == "dve":
                                return e.tensor_copy(out=o, in_=i)
                            return e.activation(out=o, in_=i, func=AF.Copy)
                        S.op(eng, ev, r=[psk(pt)], w=[xk(xs, cg * 4 + cc) for cc in range(4)])

            def store_tile(b, r0, xs):
                for sub in range(4):
                    st = alt.get()
                    for cg in range(2):
                        pt = pr6.get()

                        def tr(e, sub=sub, cg=cg, pt=pt):
                            last = None
                            for cc in range(4):
                                c = cg * 4 + cc
                                last = e.transpose(ps[:, pt, cc * 128:(cc + 1) * 128], xT[:, xs, c, sub * 128:(sub + 1) * 128],
                                                   ident_f[:, :])
                            return last
                        S.op("pe", tr, r=[xk(xs, cg * 4 + cc) for cc in range(4)] + ["consts"], w=[psk(pt)])
                        eng = "dve" if cg == 0 else "act"

                        def ev(e, st=st, cg=cg, pt=pt, eng=eng):
                            o = stg[:, st, cg * 512:(cg + 1) * 512]
                            if eng == "dve":
                                return e.tensor_copy(out=o, in_=ps[:, pt, :])
                            return e.activation(out=o, in_=ps[:, pt, :], func=AF.Copy)
                        S.op(eng, ev, r=[psk(pt)], w=[("stg", st)])
                    S.op("sp", lambda e, st=st, sub=sub: e.dma_start(out=out_d[b, r0 + sub * 128:r0 + (sub + 1) * 128, :],
                                                                     in_=stg[:, st, :]),
                         r=[("stg", st)], dma=("stgout", st))

            def kv_proj(n, koff, is_lat):
                for half in range(2):
                    slot = ws.acquire(("in", K_OFF + half * 512))
                    for hh in range(4):
                        h = half * 4 + hh
                        pk = pr6.get()
                        proj_fm(slot, hh * 128, n, pk, split=(h == 0))
                        dest = KT[:, h, koff:koff + n]
                        if is_lat:
                            rope_to(pk, n, dest, [("KT", h)])
                        else:
                            S.op("act", lambda e, pk=pk, dest=dest: e.activation(out=dest, in_=ps[:, pk, 0:n], func=AF.Copy),
                                 r=[psk(pk)], w=[("KT", h)])
                    ws.release()
                for half in range(2):
                    slot = ws.acquire(("in", V_OFF + half * 512))
                    w8 = sl8(slot)
                    for sub in range(n // 128):
                        pv = pr6.get()
                        kchunk = (koff + sub * 128) // 128

                        def mm(e, w8=w8, sub=sub, pv=pv):
                            last = None
                            for kc in range(8):
                                last = e.matmul(ps[:, pv, :], lhsT=hT[:, kc, sub * 128:(sub + 1) * 128], rhs=w8[:, kc, :],
                                                start=(kc == 0), stop=(kc == 7))
                            return last
                        S.op("pe", mm, r=[("ring", slot)] + hT_all, w=[psk(pv)])
                        eng = "dve" if sub % 2 == 0 else "act"

                        def ev(e, pv=pv, kchunk=kchunk, half=half, eng=eng):
                            o = Vb[:, kchunk, half * 520:(half + 1) * 520].rearrange("p (h e) -> p h e", e=130)[:, :, 0:128]
                            i = ps[:, pv, :].rearrange("p (h e) -> p h e", e=128)
                            if eng == "dve":
                                return e.tensor_copy(out=o, in_=i)
                            return e.activation(out=o, in_=i, func=AF.Copy)
                        S.op(eng, ev, r=[psk(pv)], w=["V"])
                    ws.release()

            def u_halo(i):
                slot = ws.acquire(("in", P_OFF))
                w8 = sl8(slot)
                pu = pr6.get()

                def mm(e):
                    last = None
                    for g in range(4):
                        for side in range(2):
                            t0 = 0 if side == 0 else TT - 8
                            for kc in range(8):
                                last = e.matmul(ps[:, pu, g * 16 + side * 8:g * 16 + side * 8 + 8],
                                                lhsT=w8[:, kc, g * 128:(g + 1) * 128], rhs=hT[:, kc, t0:t0 + 8],
                                                start=(kc == 0), stop=(kc == 7))
                    return last
                S.op("pe", mm, r=[("ring", slot)] + hT_all, w=[psk(pu)])
                S.op("dve", lambda e: e.tensor_copy(out=uhalo[:, i, :, :], in_=ps[:, pu, 0:64].rearrange("p (g t) -> p g t", g=4)),
                     r=[psk(pu)], w=["uhalo"])
                ws.release()

            tilec = [0]
            mods_done = [False]

            def phase_a_tile(b, is_lat, i):
                xs = tilec[0] % 2
                tilec[0] += 1
                n = TT if is_lat else CTX
                j = b if is_lat else 2
                if is_lat:
                    load_tile(x_d, b, i * TT, n, xs)
                    if i == 3:
                        prefetch_x(0)
                    if stage >= 3:
                        S.op("sp", lambda e: [e.dma_start(out=rope[:, 0, :], in_=ropec_d[:, i * TT:(i + 1) * TT]),
                                              e.dma_start(out=rope[:, 1, :], in_=ropes_d[:, i * TT:(i + 1) * TT])],
                             w=["rope"], dma="rope", ninst=2)
                else:
                    load_tile(ctx_d, b, 0, n, xs)
                if stage >= 2:
                    ffn(xs, n, 0, j, 0, stats_after=(stage >= 3))
                    if not mods_done[0]:
                        mods_done[0] = True
                        mods_part(6, 12, 1, 2)
                if stage >= 3:
                    norm_mod(xs, n, 1, j, have_stats=True)
                    kv_proj(n, (CTX + i * TT) if is_lat else 0, is_lat)
                    if is_lat:
                        u_halo(i)
                if is_lat:
                    S.op("sp", lambda e: e.dma_start(out=xsp_d[i], in_=xT[:, xs, :, :].rearrange("p c t -> p (c t)")),
                         r=[xk(xs, c) for c in range(8)], w=[("xsp", i)], dma=("spill", i))

            def subln_scale(h0, h1):
                nh = h1 - h0
                S.op("dve", lambda e: e.tensor_scalar(out=rr[:, :, h0:h1], in0=ssq[:, :, h0:h1],
                                                      scalar1=1.0 / 128, scalar2=EPS, op0=ALU.mult, op1=ALU.add),
                     r=["ssq"], w=[("rr", h0)])
                S.op("act", lambda e: e.activation(out=rr[:, :, h0:h1], in_=rr[:, :, h0:h1], func=AF.Sqrt),
                     r=[("rr", h0)], w=[("rr", h0)])
                S.op("dve", lambda e: e.reciprocal(out=rr[:, :, h0:h1], in_=rr[:, :, h0:h1]), r=[("rr", h0)], w=[("rr", h0)])
                av = attn_tok[:, :, h0:h1, :]
                S.op("dve", lambda e: e.tensor_tensor(out=av, in0=av, in1=rr[:, :, h0:h1].unsqueeze(3).broadcast_to([128, 4, nh, 128]),
                                                      op=ALU.mult), r=ATK + [("rr", h0)], w=ATK)
                S.op("dve", lambda e: e.tensor_tensor(out=av, in0=av,
                                                      in1=gs_bc[:, :].unsqueeze(1).unsqueeze(1).broadcast_to([128, 4, nh, 128]),
                                                      op=ALU.mult), r=ATK + ["gs_bc"], w=ATK)

            def subln_pe(h0, h1):
                for h in range(h0, h1):
                    pt = 7 if h0 == 0 else pr6.get()

                    def tr(e, h=h, pt=pt):
                        last = None
                        for qs in range(4):
                            last = e.transpose(psb[:, pt, qs * 128:(qs + 1) * 128], attn_tok[:, qs, h, :], ident_b[:, :])
                        return last
                    S.op("pe", tr, r=ATK + ["ident_b"], w=[psk(pt)])
                    if h0 == 0 or h % 2 == 0:
                        S.op("dve", lambda e, h=h, pt=pt: e.tensor_copy(out=attnT[:, h, :], in_=psb[:, pt, 0:512]),
                             r=[psk(pt)], w=[("attnT", h)])
                    else:
                        S.op("act", lambda e, h=h, pt=pt: e.activation(out=attnT[:, h, :], in_=psb[:, pt, 0:512], func=AF.Copy),
                             r=[psk(pt)], w=[("attnT", h)])

            def attention(xs, i):
                uslot, pslot = pool_begin()
                sp_rot = Rot(2)
                steps = [(h, kc) for h in range(NH) for kc in range(NKC)]
                sb_of = {}

                def emit_qk(h, kc):
                    sb0 = 2 * sp_rot.get()
                    sb_of[(h, kc)] = sb0

                    def qk(e, h=h, kc=kc, sb0=sb0):
                        e.matmul(ps[:, sb0, :], lhsT=KT[0:64, h, kc * 128:(kc + 1) * 128], rhs=qT[0:64, h, :],
                                 start=True, stop=True)
                        return e.matmul(ps[:, sb0 + 1, :], lhsT=KT[64:128, h, kc * 128:(kc + 1) * 128],
                                        rhs=qT[64:128, h, :], start=True, stop=True)
                    S.op("pe", qk, r=[("KT", h), ("R", h)], w=[psk(sb0), psk(sb0 + 1)])

                emit_qk(*steps[0])
                for si, (h, kc) in enumerate(steps):
                    if si + 1 < len(steps):
                        emit_qk(*steps[si + 1])
                    sb0 = sb_of[(h, kc)]
                    esl = kc % 2
                    if MERGE_EXP:
                        S.op("act", lambda e, sb0=sb0, esl=esl: e.activation(out=Ebuf[:, esl, :, :], in_=ps[:, sb0:sb0 + 2, :],
                                                                             func=AF.Exp, scale=0.125),
                             r=[psk(sb0), psk(sb0 + 1)], w=[("R", 16 + 2 * esl), ("R", 17 + 2 * esl)])
                    else:
                        for c in range(2):
                            S.op("act", lambda e, sb0=sb0, esl=esl, c=c: e.activation(out=Ebuf[:, esl, c, :], in_=ps[:, sb0 + c, :],
                                                                                      func=AF.Exp, scale=0.125),
                                 r=[psk(sb0 + c)], w=[("R", 16 + 2 * esl + c)])

                    def pv(e, h=h, kc=kc, esl=esl):
                        last = None
                        for qs in range(4):
                            for c in range(2):
                                a = qs * 2 + c
                                bank, off = 4 + a // 3, (a % 3) * 130
                                last = e.matmul(ps[:, bank, off:off + 130], lhsT=Ebuf[:, esl, c, qs * 128:(qs + 1) * 128],
                                                rhs=Vb[:, kc, h * 130:(h + 1) * 130], start=(kc == 0 and a % 3 == 0),
                                                stop=(kc == NKC - 1), skip_group_check=True)
                        return last
                    S.op("pe", pv, r=[("R", 16 + 2 * esl), ("R", 17 + 2 * esl), "V"], w=[psk(4), psk(5), psk(6)])
                    if h < 4 and kc == 2:
                        pool_a(i, h, uslot)
                    if h < 4 and kc == 12:
                        pool_b(h, pslot)
                    if h == 4 and kc == 0:
                        pool_end()
                    if h == 4 and kc == 6:
                        subln_pe(0, 4)
                    if kc != NKC - 1:
                        continue
                    S.op("dve", lambda e: e.tensor_copy(out=Oev[:, 0:3, :], in_=ps[:, 4, 0:390].rearrange("p (a e) -> p a e", a=3)),
                         r=[psk(4)], w=["Oev"])
                    S.op("dve", lambda e: e.tensor_copy(out=Oev[:, 3:6, :], in_=ps[:, 5, 0:390].rearrange("p (a e) -> p a e", a=3)),
                         r=[psk(5), "Oev"], w=["Oev"])
                    S.op("dve", lambda e: e.tensor_copy(out=Oev[:, 6:8, :], in_=ps[:, 6, 0:260].rearrange("p (a e) -> p a e", a=2)),
                         r=[psk(6), "Oev"], w=["Oev"])
                    S.op("dve", lambda e: e.reciprocal(out=rsO[:, :], in_=Oev[:, :, 128]), r=["Oev"], w=["rsO"])
                    S.op("dve", lambda e: e.tensor_tensor(out=Oev[:, :, 0:128], in0=Oev[:, :, 0:128],
                                                          in1=rsO[:, :].unsqueeze(2).broadcast_to([128, 8, 128]), op=ALU.mult),
                         r=["Oev", "rsO"], w=["Oev"])
                    Ov = Oev[:, :, :].rearrange("p (q c) e -> p q c e", c=2)
                    S.op("dve", lambda e, Ov=Ov: e.scalar_tensor_tensor(out=Ov[:, :, 0, 0:128], in0=Ov[:, :, 1, 0:128], scalar=nlam,
                                                                        in1=Ov[:, :, 0, 0:128], op0=ALU.mult, op1=ALU.add),
                         r=["Oev", "lsm"], w=["Oev"])
                    s = sc.get()
                    sv = scr[:, s, 0:512].rearrange("p (q e) -> p q e", q=4)
                    S.op("dve", lambda e, Ov=Ov, sv=sv: e.tensor_tensor(out=sv, in0=Ov[:, :, 0, 0:128], in1=Ov[:, :, 0, 0:128],
                                                                        op=ALU.mult), r=["Oev"], w=[("scr", s)])
                    S.op("dve", lambda e, sv=sv, h=h: e.tensor_reduce(out=ssq[:, :, h], in_=sv, axis=AX.X, op=ALU.add),
                         r=[("scr", s)], w=["ssq"])
                    S.op("dve", lambda e, Ov=Ov, h=h: e.tensor_copy(out=attn_tok[:, :, h, :], in_=Ov[:, :, 0, 0:128]),
                         r=["Oev"], w=ATK)
                    if h == 3:
                        subln_scale(0, 4)
                subln_scale(4, 8)
                subln_pe(4, 8)

            def pool_begin():
                uslot = ws.acquire(("in", P_OFF))
                pslot = ws.acquire(("wpool",))
                return uslot, pslot

            def pool_a(i, g, uslot):
                pu = 7
                proj_fm(uslot, g * 128, TT, pu)
                S.op("dve", lambda e: e.tensor_copy(out=ug[:, 8:520], in_=ps[:, pu, :]), r=[psk(pu)], w=["ug"])
                if i > 0:
                    S.op("dve", lambda e: e.tensor_copy(out=ug[:, 0:8], in_=uhalo[:, i - 1, g, 8:16]),
                         r=["uhalo", "ug"], w=["ug"])
                else:
                    S.op("dve", lambda e: e.memset(ug[:, 0:8], 0.0), r=["ug"], w=["ug"])
                if i < 3:
                    S.op("dve", lambda e: e.tensor_copy(out=ug[:, 520:528], in_=uhalo[:, i + 1, g, 0:8]),
                         r=["uhalo", "ug"], w=["ug"])
                else:
                    S.op("dve", lambda e: e.memset(ug[:, 520:528], 0.0), r=["ug"], w=["ug"])
                w = 2 ** (g + 1)
                lo, hi, sh = 1, 528, 1
                prev = sc.get()
                S.op("dve", lambda e, s=prev: e.tensor_tensor(out=scr[:, s, 1:528], in0=ug[:, 0:527], in1=ug[:, 1:528], op=ALU.add),
                     r=["ug"], w=[("scr", prev)])
                step = 2
                while step < w:
                    s2 = sc.get()
                    nlo, nhi = lo + sh, hi - sh

                    def dbl(e, s2=s2, p=prev, nlo=nlo, nhi=nhi, sh=sh):
                        return e.tensor_tensor(out=scr[:, s2, nlo:nhi], in0=scr[:, p, nlo - sh:nhi - sh],
                                               in1=scr[:, p, nlo + sh:nhi + sh], op=ALU.add)
                    S.op("dve", dbl, r=[("scr", prev)], w=[("scr", s2)])
                    lo, hi, prev = nlo, nhi, s2
                    step *= 2
                    sh *= 2
                assert lo <= 8 and hi >= 520, (lo, hi)
                pk_ = ("pbuf", g % 2)
                pbv = pbuf[:, g % 2, :]
                S.op("dve", lambda e, p=prev: e.scalar_tensor_tensor(out=pbv, in0=scr[:, p, 8:520], scalar=1.0 / w, in1=ug[:, 8:520],
                                                                   op0=ALU.mult, op1=ALU.subtract),
                     r=[("scr", prev), "ug"], w=[pk_])
                for side, cond in ((0, i == 0), (1, i == 3)):
                    if not cond:
                        continue
                    a0 = 8 if side == 0 else 512
                    o0 = 0 if side == 0 else 504
                    s3 = sc.get()
                    S.op("dve", lambda e, p=prev, s3=s3, a0=a0, side=side: e.tensor_tensor(
                        out=scr[:, s3, 0:8], in0=scr[:, p, a0:a0 + 8], in1=edge[:, g, side * 8:side * 8 + 8], op=ALU.mult),
                        r=[("scr", prev), "consts"], w=[("scr", s3)])
                    S.op("dve", lambda e, s3=s3, a0=a0, o0=o0: e.tensor_tensor(
                        out=pbuf[:, g % 2, o0:o0 + 8], in0=scr[:, s3, 0:8], in1=ug[:, a0:a0 + 8], op=ALU.subtract),
                        r=[("scr", s3), "ug", pk_], w=[pk_])

            def pool_b(g, pslot):
                wp = ring[:, pslot, 0:512].rearrange("p (g n) -> p g n", g=4)
                pp = 7
                S.op("pe", lambda e: e.matmul(ps[:, pp, :], lhsT=wp[:, g, :], rhs=pbuf[:, g % 2, :], start=True, stop=True),
                     r=[("ring", pslot), ("pbuf", g % 2)], w=[psk(pp)])
                S.op("dve", lambda e: e.tensor_scalar_mul(out=poolT[:, g, :], in0=ps[:, pp, :], scalar1=vecT[:, 104 + g:105 + g]),
                     r=[psk(pp), "vecT"], w=["rope"])

            def pool_end():
                ws.release()
                ws.release()

            def merge(xs, j, stats_after=False):
                attnT_all = [("attnT", h) for h in range(8)]
                poolT_all = ["rope"]
                for pr in range(4):
                    gslot = ws.acquire(("G", pr))
                    bslot = ws.acquire(("B", pr))
                    g8 = sl8(gslot)
                    ba8 = ring[:, bslot, 0:2048].rearrange("p (kc n) -> p kc n", kc=8)
                    bp4 = ring[:, bslot, 2048:3072].rearrange("p (kc n) -> p kc n", kc=4)
                    for dd in range(2):
                        d = pr * 2 + dd
                        pga, pgp, pza, pzp = pr6.get(), pr6.get(), pr6.get(), pr6.get()

                        def mm8(e, w, col, rhs, pk, nk=8):
                            last = None
                            for kc in range(nk):
                                last = e.matmul(ps[:, pk, :], lhsT=w[:, kc, col:col + 128], rhs=rhs[:, kc, :],
                                                start=(kc == 0), stop=(kc == nk - 1))
                            return last
                        S.op("pe", lambda e, dd=dd, pga=pga, g8=g8: mm8(e, g8, dd * 128, hT, pga),
                             r=[("ring", gslot)] + hT_all, w=[psk(pga)])
                        S.op("pe", lambda e, dd=dd, pgp=pgp, g8=g8: mm8(e, g8, 256 + dd * 128, hT, pgp),
                             r=[("ring", gslot)] + hT_all, w=[psk(pgp)])
                        sa, sp_, s1, s2 = sc.get(), sc.get(), sc.get(), sc.get()
                        S.op("act", lambda e, pga=pga, sa=sa: e.activation(out=scr[:, sa, 0:512], in_=ps[:, pga, :], func=AF.Tanh,
                                                                          scale=0.5), r=[psk(pga)], w=[("scr", sa)])
                        S.op("act", lambda e, pgp=pgp, sp_=sp_: e.activation(out=scr[:, sp_, 0:512], in_=ps[:, pgp, :], func=AF.Tanh,
                                                                            scale=0.5), r=[psk(pgp)], w=[("scr", sp_)])
                        S.op("pe", lambda e, dd=dd, pza=pza, ba8=ba8: mm8(e, ba8, dd * 128, attnT, pza),
                             r=[("ring", bslot)] + attnT_all, w=[psk(pza)])
                        S.op("pe", lambda e, dd=dd, pzp=pzp, bp4=bp4: mm8(e, bp4, dd * 128, poolT, pzp, 4),
                             r=[("ring", bslot)] + poolT_all, w=[psk(pzp)])
                        S.op("dve", lambda e, sa=sa, pza=pza, s1=s1: e.scalar_tensor_tensor(
                            out=scr[:, s1, 0:512], in0=scr[:, sa, 0:512], scalar=1.0, in1=ps[:, pza, :], op0=ALU.add, op1=ALU.mult),
                            r=[("scr", sa), psk(pza)], w=[("scr", s1)])
                        S.op("dve", lambda e, sp_=sp_, pzp=pzp, s2=s2: e.scalar_tensor_tensor(
                            out=scr[:, s2, 0:512], in0=scr[:, sp_, 0:512], scalar=1.0, in1=ps[:, pzp, :], op0=ALU.add, op1=ALU.mult),
                            r=[("scr", sp_), psk(pzp)], w=[("scr", s2)])
                        S.op("dve", lambda e, d=d, s1=s1, s2=s2: e.tensor_tensor(out=yT[:, d, :], in0=scr[:, s1, 0:512],
                                                                                in1=scr[:, s2, 0:512], op=ALU.add),
                             r=[("scr", s1), ("scr", s2)], w=[("R", d)])
                    ws.release()
                    ws.release()
                yT_all = [("R", d) for d in range(8)]
                for half in range(2):
                    slot = ws.acquire(("out", half))
                    w8 = sl8(slot)
                    for dd in range(4):
                        d = half * 4 + dd
                        po = pr6.get()

                        def mmo(e, w8=w8, dd=dd, po=po):
                            last = None
                            for kc in range(8):
                                last = e.matmul(ps[:, po, :], lhsT=w8[:, kc, dd * 128:(dd + 1) * 128], rhs=yT[:, kc, :],
                                                start=(kc == 0), stop=(kc == 7))
                            return last
                        S.op("pe", mmo, r=[("ring", slot)] + yT_all, w=[psk(po)])
                        if stats_after and d > 0:
                            stat_mm(TT, d - 1, pend)
                        S.op("dve", lambda e, d=d, po=po: e.scalar_tensor_tensor(out=xT[:, xs, d, :], in0=ps[:, po, :],
                                                                                scalar=GG[:, 1, d, j:j + 1], in1=xT[:, xs, d, :],
                                                                                op0=ALU.mult, op1=ALU.add),
                             r=[psk(po), "GG", xk(xs, d)], w=[xk(xs, d)])
                        if stats_after:
                            pend = stat_sq(xs, TT, d, "act")
                    ws.release()
                if stats_after:
                    stat_mm(TT, 7, pend)
                    stat_fin(TT)

            def prefetch_x(i_next):
                xs_n = tilec[0] % 2
                S.op("sp", lambda e: e.dma_start(out=xT[:, xs_n, :, :].rearrange("p c t -> p (c t)"), in_=xsp_d[i_next]),
                     r=[("xsp", i_next)], w=[xk(xs_n, c) for c in range(8)], dma=("xload", xs_n))

            def phase_b_tile(b, i):
                xs = tilec[0] % 2
                tilec[0] += 1
                j = b
                if i < 3:
                    prefetch_x(i + 1)
                if stage >= 3.2:
                    S.op("sp", lambda e: [e.dma_start(out=rope[:, 0, :], in_=ropec_d[:, i * TT:(i + 1) * TT]),
                                          e.dma_start(out=rope[:, 1, :], in_=ropes_d[:, i * TT:(i + 1) * TT])],
                         w=["rope"], dma="rope", ninst=2)
                    norm_mod(xs, TT, 1, j)
                    for half in range(2):
                        slot = ws.acquire(("in", half * 512))
                        for hh in range(4):
                            h = half * 4 + hh
                            pk = pr6.get()
                            proj_fm(slot, hh * 128, TT, pk, split=(h == 0))
                            rope_to(pk, TT, qT[:, h, :], [("R", h)])
                        ws.release()
                    if stage >= 3.4:
                        attention(xs, i)
                    if stage >= 3.5:
                        merge(xs, j, stats_after=(stage >= 4))
                if stage >= 4:
                    ffn(xs, TT, 1, j, 2, have_stats=True, stats_after=True)
                    for c in range(8):
                        s = sc.get()
                        S.op("dve", lambda e, c=c, s=s: e.tensor_tensor(out=scr[:, s, 0:TT], in0=xT[:, xs, c, :], in1=rstd[:, :],
                                                                        op=ALU.mult), r=[xk(xs, c), "rstd"], w=[("scr", s)])
                        S.op("act", lambda e, c=c, s=s: e.activation(out=xT[:, xs, c, :], in_=scr[:, s, 0:TT], func=AF.Identity,
                                                                     scale=vecT[:, 96 + c:97 + c]),
                             r=[("scr", s), "vecT"], w=[xk(xs, c)])
                store_tile(b, i * TT, xs)

            for b in range(NB):
                if stage >= 2:
                    phase_a_tile(b, False, 0)
                for i in range(4):
                    phase_a_tile(b, True, i)
                if b == 0 and stage >= 2:
                    mods_part(12, 18, 2, 3)
                for i in range(4):
                    phase_b_tile(b, i)

        S0 = Sched()
        ws0 = WStream(S0, fill_fn)
        program(S0, ws0)
        tags = ws0.tags
        img_tags.clear()
        S = Sched()
        ws = WStream(S, fill_fn, tags)
        program(S, ws)
        assert ws.na == len(tags) and ws.nf == len(tags), (ws.na, ws.nf, len(tags))
        cnt = S.analyse()
        sems = {}
        for key in cnt:
            nm = "s_" + "_".join(str(x) for x in (key[1] if isinstance(key[1], tuple) else (key[1],)))
            sems[key] = es.enter_context(nc.semaphore(nm))
        S.emit(nc, {"pe": None, "act": None, "dve": None, "pool": None, "sp": None}, sems)
        with nc.Block() as block:
            @block.tensor
            def _(e):
                S.emit_engine("pe", e, sems)

            @block.scalar
            def _(e):
                S.emit_engine("act", e, sems)

            @block.vector
            def _(e):
                S.emit_engine("dve", e, sems)

            @block.gpsimd
            def _(e):
                S.emit_engine("pool", e, sems)

            @block.sync
            def _(e):
                S.emit_engine("sp", e, sems)
    return nc


def _consts():
    half = 16
    freqs = (10000.0 ** (-np.arange(half, dtype=np.float32) / half)).astype(np.float32)
    t = np.arange(L)
    row = (t // 64).astype(np.float32)
    col = (t % 64).astype(np.float32)
    ropec = np.zeros((128, L), np.float32)
    ropes = np.zeros((128, L), np.float32)
    perm = np.zeros((128, 128), np.float32)
    for p in range(128):
        d = p % 64
        axis = d // 32
        jj = d % 32
        pos = row if axis == 0 else col
        ang = (pos * freqs[jj % 16]).astype(np.float32)
        ropec[p] = np.cos(ang)
        if jj < 16:
            ropes[p] = -np.sin(ang)
            partner = p + 16
        else:
            ropes[p] = np.sin(ang)
            partner = p - 16
        perm[partner, p] = 1.0
    edge = np.zeros((4, 16), np.float32)
    for g, w in enumerate((2, 4, 8, 16)):
        for tt in range(8):
            lo = max(tt - w // 2, 0)
            hi = min(tt + w - w // 2, L)
            edge[g, tt] = 1.0 / (hi - lo)
            t2 = L - 8 + tt
            lo = max(t2 - w // 2, 0)
            hi = min(t2 + w - w // 2, L)
            edge[g, 8 + tt] = 1.0 / (hi - lo)
    edge_bc = np.ascontiguousarray(np.broadcast_to(edge.reshape(1, 64), (128, 64)))
    return ropec, ropes, perm, np.eye(128, dtype=np.float32), edge_bc


def make_in_maps(inp, ncores=NCORES):
    f = lambda a: np.ascontiguousarray(np.asarray(a, dtype=np.float32))
    ropec, ropes, perm, ident, edge_bc = _consts()
    vec1 = np.concatenate([f(inp["b_mod"]).reshape(72, 128), f(inp["g_norm"]).reshape(24, 128),
                           f(inp["g_final"]).reshape(8, 128), f(inp["pool_scale"]).reshape(4, 128)], axis=0)
    lam = np.concatenate([f(inp["lambda_q1"]).reshape(1, 64), f(inp["lambda_k1"]).reshape(1, 64),
                          f(inp["lambda_q2"]).reshape(1, 64), f(inp["lambda_k2"]).reshape(1, 64)], axis=1)
    lam_bc = np.ascontiguousarray(np.broadcast_to(lam, (128, 256)))
    gsub_bc = np.ascontiguousarray(np.broadcast_to(f(inp["g_subln"]).reshape(1, 128), (128, 128)))
    shared = {
        "vec1": np.ascontiguousarray(vec1), "w_mod": f(inp["w_mod"])[0], "w_gu": f(inp["w_ffn_gu"])[0],
        "w_down": f(inp["w_ffn_down"])[0], "w_in": f(inp["w_in"])[0], "w_ba": f(inp["w_branch_attn"])[0],
        "w_bp": f(inp["w_branch_pool"])[0], "w_out": f(inp["w_out"])[0], "w_pool": f(inp["w_pool"])[0],
        "lam_bc": lam_bc, "gsub_bc": gsub_bc, "ropec": ropec, "ropes": ropes, "ident": ident, "perm": perm,
        "edge": edge_bc,
    }
    x = f(inp["x"])
    ctx = f(inp["ctx"])
    c = f(inp["c"])
    c_ctx = f(inp["c_ctx"])
    maps = []
    for k in range(ncores):
        cc = np.concatenate([c[NB * k:NB * k + NB], c_ctx.reshape(1, D)], axis=0).reshape(24, 128)
        m = dict(shared)
        m["x"] = np.ascontiguousarray(x[NB * k:NB * k + NB])
        m["ctx"] = np.ascontiguousarray(ctx[NB * k:NB * k + NB])
        m["cc"] = np.ascontiguousarray(cc)
        maps.append(m)
    return maps


_NC_CACHE = {}


def run(inp, stage=4, ncores=NCORES, trace=False):
    if stage not in _NC_CACHE:
        _NC_CACHE[stage] = build_nc(stage)
    nc = _NC_CACHE[stage]
    maps = make_in_maps(inp, ncores)
    res = run_bass_kernel_spmd(nc, maps, core_ids=list(range(ncores)), trace=trace)
    out = np.concatenate([r["out"] for r in res.results], axis=0)
    return out, res


def kernel(**inputs):
    out, _ = run(inputs, stage=4, ncores=NCORES)
    return out.astype(np.float32)
```

```python
from contextlib import ExitStack
import math
import numpy as np
import concourse.bass as bass
import concourse.mybir as mybir
from concourse.bass_utils import run_bass_kernel_spmd

F32 = mybir.dt.float32
BF16 = mybir.dt.bfloat16
AF = mybir.ActivationFunctionType
ALU = mybir.AluOpType
AX = mybir.AxisListType

D = 1024
L = 2048
CTX = 256
NH = 8
DFF = 2816
NFC = 22
LK = CTX + L
NKC = LK // 128
K_OFF, V_OFF, P_OFF, G_OFF = 1024, 2048, 3072, 3584
EPS = 1e-6
LAM_INIT = 0.8 - 0.6 * math.exp(-0.3 * 0)
NSCR = 6
RING = 3
TT = 512
NCORES = 8
NB = 2
MERGE_EXP = True
POOL_POW = False
LN_EXP = True
NIMG = 56


class Op:
    __slots__ = ("eng", "fn", "r", "w", "dma", "deps", "signal", "cnt", "ninst")

    def __init__(self, eng, fn, r, w, dma):
        self.eng, self.fn, self.r, self.w, self.dma = eng, fn, r, w, dma
        self.deps = None
        self.signal = False
        self.cnt = 0
        self.ninst = 1


class Sched:
    def __init__(self):
        self.ops = []

    def op(self, eng, fn, r=(), w=(), dma=None, ninst=1):
        pr = [k for k in r if isinstance(k, tuple) and k[0] == "ps"]
        if pr:
            r = [k for k in r if k not in pr]
            w = list(w) + pr
        o = Op(eng, fn, tuple(r), tuple(w), dma)
        o.ninst = ninst
        self.ops.append(o)

    def analyse(self):
        last_w = {}
        readers = {}
        ops = self.ops
        for i, o in enumerate(ops):
            deps = set()
            for k in o.r:
                j = last_w.get(k)
                if j is not None:
                    deps.add(j)
            for k in o.w:
                j = last_w.get(k)
                if j is not None:
                    deps.add(j)
                rd = readers.get(k)
                if rd:
                    deps.update(rd.values())
            deps.discard(i)
            if o.eng == "pe":
                deps = {j for j in deps if ops[j].eng != "pe" or ops[j].dma}
            o.deps = deps
            for j in deps:
                ops[j].signal = True
            for k in o.r:
                rd = readers.setdefault(k, {})
                rd[o.dma if o.dma else o.eng] = i
            for k in o.w:
                last_w[k] = i
                readers[k] = {}
        cnt = {}
        for o in ops:
            key = ("d", o.dma) if o.dma else ("e", o.eng)
            if o.dma:
                cnt[key] = cnt.get(key, 0) + 16 * o.ninst
                o.cnt = cnt[key]
            elif o.signal:
                cnt[key] = cnt.get(key, 0) + 1
                o.cnt = cnt[key]
        self.final = cnt
        return cnt

    def emit(self, nc, engines, sems):
        ops = self.ops
        for ename, eng in engines.items():
            pass
        streams = {e: [] for e in engines}
        for i, o in enumerate(ops):
            streams[o.eng].append(i)
        self._streams = streams

    def emit_engine(self, ename, eng, sems):
        ops = self.ops
        known = {}
        for i in self._streams[ename]:
            o = ops[i]
            need = {}
            for j in o.deps:
                d = ops[j]
                key = ("d", d.dma) if d.dma else ("e", d.eng)
                if d.cnt > need.get(key, 0):
                    need[key] = d.cnt
            for key, v in need.items():
                if known.get(key, 0) >= v:
                    continue
                eng.wait_ge(sems[key], v)
                known[key] = v
            ins = o.fn(eng)
            if o.dma:
                if not isinstance(ins, (list, tuple)):
                    ins = [ins]
                assert len(ins) == o.ninst, (len(ins), o.ninst)
                for x in ins:
                    x.then_inc(sems[("d", o.dma)], 16)
            elif o.signal:
                if isinstance(ins, (list, tuple)):
                    ins = ins[-1]
                ins.then_inc(sems[("e", o.eng)], 1)
        if ename == "sp":
            for key, v in self.final.items():
                if known.get(key, 0) < v:
                    eng.wait_ge(sems[key], v)


class Rot:
    def __init__(self, n, start=0):
        self.n, self.i, self.s = n, 0, start

    def get(self):
        v = self.s + (self.i % self.n)
        self.i += 1
        return v


class WStream:
    def __init__(self, S, fill_fn, tags=None):
        self.S, self.fill_fn = S, fill_fn
        self.record = tags is None
        self.tags = [] if tags is None else tags
        self.na = 0
        self.nf = 0
        self.held = []
        if not self.record:
            for _ in range(min(RING, len(self.tags))):
                self._fill()

    def _fill(self):
        k = self.nf
        self.fill_fn(self.S, self.tags[k], k % RING)
        self.nf += 1

    def acquire(self, tag):
        k = self.na
        self.na += 1
        if self.record:
            self.tags.append(tag)
        else:
            assert self.tags[k] == tag, (k, self.tags[k], tag)
        self.held.append(k)
        return k % RING

    def release(self):
        self.held.pop(0)
        if not self.record and self.nf < len(self.tags):
            assert all(h > self.nf - RING for h in self.held) or not self.held, (self.held, self.nf)
            self._fill()


def build_nc(stage=4):
    nc = bass.Bass("TRN2", target_bir_lowering=False)

    def din(name, shape):
        return nc.dram_tensor(name, list(shape), F32, kind="ExternalInput").ap()

    x_d = din("x", [NB, L, D])
    ctx_d = din("ctx", [NB, CTX, D])
    cc_d = din("cc", [24, 128])
    vec1_d = din("vec1", [108, 128])
    wmod_d = din("w_mod", [D, 9 * D])
    wgu_d = din("w_gu", [2, D, 2 * DFF])
    wdn_d = din("w_down", [2, DFF, D])
    win_d = din("w_in", [D, 5632])
    wba_d = din("w_ba", [D, D])
    wbp_d = din("w_bp", [512, D])
    wout_d = din("w_out", [D, D])
    wpool_d = din("w_pool", [4, 128, 128])
    lam_d = din("lam_bc", [128, 256])
    gsub_d = din("gsub_bc", [128, 128])
    ropec_d = din("ropec", [128, L])
    ropes_d = din("ropes", [128, L])
    ident_d = din("ident", [128, 128])
    perm_d = din("perm", [128, 128])
    edge_d = din("edge", [128, 64])
    out_d = nc.dram_tensor("out", [NB, L, D], F32, kind="ExternalOutput").ap()
    xsp_d = nc.dram_tensor("xsp", [4, 128, 8 * TT], F32, kind="Internal").ap()
    wimg_d = nc.dram_tensor("wimg", [NIMG, 128, 4096], BF16, kind="Internal").ap()

    es = ExitStack()
    with es:
        def sb(name, shape, dt):
            return es.enter_context(nc.sbuf_tensor("sb_" + name, list(shape), dt))

        KT = sb("KT", [128, NH, LK], BF16)
        Vb = sb("Vb", [128, NKC, NH * 130], BF16)
        xT = sb("xT", [128, 2, 8, TT], F32)
        hT = sb("hT", [128, 8, TT], BF16)
        R22 = sb("R22", [128, NFC * TT], BF16)
        attnT = sb("attnT", [128, 8, TT], BF16)
        stg = sb("stg", [128, 2, D], F32)
        ring = sb("ring", [128, RING, 4096], BF16)
        rope = sb("rope", [128, 2, TT], F32)
        Oev = sb("Oev", [128, 8, 130], F32)
        scr = sb("scr", [128, NSCR, 528], F32)
        ug = sb("ug", [128, 528], F32)
        pbuf = sb("pbuf", [128, 2, TT], BF16)
        rstd = sb("rstd", [128, TT], F32)
        uhalo = sb("uhalo", [128, 4, 4, 16], F32)
        ident_f = sb("ident_f", [128, 128], F32)
        ident_b = sb("ident_b", [128, 128], BF16)
        perm_b = sb("perm_b", [128, 128], BF16)
        ones_f = sb("ones_f", [128, 128], F32)
        vecT = sb("vecT", [128, 108], F32)
        ccT = sb("ccT", [128, 24], F32)
        scT = sb("scT", [128, 24], BF16)
        modT = sb("modT", [128, 72, 3], F32)
        AA = sb("AA", [128, 3, 8, 3], F32)
        GG = sb("GG", [128, 3, 8, 3], F32)
        lsm = sb("lsm", [128, 8], F32)
        gs_bc = sb("gs_bc", [128, 128], F32)
        edge = sb("edge", [128, 4, 16], F32)
        rsO = sb("rsO", [128, 8], F32)
        chalf = sb("chalf", [128, 1], F32)
        ssq = sb("ssq", [128, 4, 8], F32)
        rr = sb("rr", [128, 4, 8], F32)
        ps = es.enter_context(nc.psum_tensor("ps", [128, 8, 512], F32))
        psb = ps.bitcast(BF16)

        actT = R22[:, :].rearrange("p (f t) -> p f t", t=TT)
        qT = R22[:, 0:8 * TT].rearrange("p (f t) -> p f t", t=TT)
        yT = qT
        attn_tok = R22[:, 8 * TT:16 * TT].rearrange("p (q h e) -> p q h e", q=4, h=8)
        Ebuf = R22[:, 16 * TT:20 * TT].rearrange("p (s c t) -> p s c t", s=2, c=2)
        scrb = scr.bitcast(BF16)
        poolT = rope.bitcast(BF16)[:, :, :].rearrange("p a (b t) -> p (a b) t", t=TT)
        perm_f = scr[:, 1, 0:128]
        vec1s = scr[:, 2, 0:128]
        ccs = scr[:, 3, 0:128]
        lamb = scr[:, 4, 0:256]
        modv = modT[:, :, :].rearrange("p (a k c) j -> p a k c j", a=3, k=3, c=8)

        def psk(b):
            return ("ps", b)

        img_tags = {}

        def fill_fn(S, tag, slot):
            kind = tag[0]
            rk = ("ring", slot)
            sl = ring[:, slot, :]
            if kind != "mod":
                if tag in img_tags:
                    k = img_tags[tag]
                    S.op("pool", lambda g: g.dma_start(out=sl, in_=wimg_d[k]), r=[("img", k)], w=[rk], dma=rk)
                    return
                fill_fp32(S, tag, slot)
                k = len(img_tags)
                img_tags[tag] = k
                S.op("sp", lambda e: e.dma_start(out=wimg_d[k], in_=sl), r=[rk], w=[("img", k)], dma=("wb", slot))
                return
            fill_fp32(S, tag, slot)

        def fill_fp32(S, tag, slot):
            kind = tag[0]
            rk = ("ring", slot)
            sl = ring[:, slot, :]
            if kind == "mod":
                k = tag[1]
                src = wmod_d[:, k * 512:(k + 1) * 512].rearrange("(kc p) n -> p kc n", p=128)
                S.op("pool", lambda g: g.dma_start(out=sl.rearrange("p (kc n) -> p kc n", kc=8), in_=src),
                     w=[rk], dma=rk)
            elif kind == "gu":
                _, l, i = tag
                sa = wgu_d[l, :, i * 256:(i + 1) * 256].rearrange("(kc p) n -> p kc n", p=128)
                sbb = wgu_d[l, :, DFF + i * 256:DFF + (i + 1) * 256].rearrange("(kc p) n -> p kc n", p=128)
                dv = sl.rearrange("p (kc n) -> p kc n", kc=8)
                S.op("pool", lambda g: [g.dma_start(out=dv[:, :, 0:256], in_=sa),
                                        g.dma_start(out=dv[:, :, 256:512], in_=sbb)], w=[rk], dma=rk, ninst=2)
            elif kind == "down":
                _, l, d = tag
                src = wdn_d[l, :, d * 128:(d + 1) * 128].rearrange("(fc p) n -> p fc n", p=128)
                S.op("pool", lambda g: g.dma_start(out=sl[:, 0:NFC * 128].rearrange("p (fc n) -> p fc n", fc=NFC), in_=src),
                     w=[rk], dma=rk)
            elif kind == "in":
                c0 = tag[1]
                src = win_d[:, c0:c0 + 512].rearrange("(kc p) n -> p kc n", p=128)
                S.op("pool", lambda g: g.dma_start(out=sl.rearrange("p (kc n) -> p kc n", kc=8), in_=src),
                     w=[rk], dma=rk)
            elif kind == "G":
                pr = tag[1]
                sa = win_d[:, G_OFF + pr * 256:G_OFF + (pr + 1) * 256].rearrange("(kc p) n -> p kc n", p=128)
                sbb = win_d[:, G_OFF + D + pr * 256:G_OFF + D + (pr + 1) * 256].rearrange("(kc p) n -> p kc n", p=128)
                dv = sl.rearrange("p (kc n) -> p kc n", kc=8)
                S.op("pool", lambda g: [g.dma_start(out=dv[:, :, 0:256], in_=sa),
                                        g.dma_start(out=dv[:, :, 256:512], in_=sbb)], w=[rk], dma=rk, ninst=2)
            elif kind == "B":
                pr = tag[1]
                sa = wba_d[:, pr * 256:(pr + 1) * 256].rearrange("(kc p) n -> p kc n", p=128)
                sbb = wbp_d[:, pr * 256:(pr + 1) * 256].rearrange("(kc p) n -> p kc n", p=128)
                S.op("pool", lambda g: [g.dma_start(out=sl[:, 0:2048].rearrange("p (kc n) -> p kc n", kc=8), in_=sa),
                                        g.dma_start(out=sl[:, 2048:3072].rearrange("p (kc n) -> p kc n", kc=4), in_=sbb)],
                     w=[rk], dma=rk, ninst=2)
            elif kind == "out":
                h = tag[1]
                src = wout_d[:, h * 512:(h + 1) * 512].rearrange("(kc p) n -> p kc n", p=128)
                S.op("pool", lambda g: g.dma_start(out=sl.rearrange("p (kc n) -> p kc n", kc=8), in_=src),
                     w=[rk], dma=rk)
            elif kind == "wpool":
                src = wpool_d[:, :, :].rearrange("g p n -> p g n")
                S.op("pool", lambda g: g.dma_start(out=sl[:, 0:512].rearrange("p (g n) -> p g n", g=4), in_=src),
                     w=[rk], dma=rk)
            else:
                raise ValueError(tag)

        def program(S, ws):
            pr6 = Rot(6)
            sc = Rot(NSCR)
            alt = Rot(2)

            def sl8(slot):
                return ring[:, slot, :].rearrange("p (kc n) -> p kc n", kc=8)

            S.op("sp", lambda e: [e.dma_start(out=ident_f[:, :], in_=ident_d[:, :]),
                                  e.dma_start(out=perm_f, in_=perm_d[:, :]),
                                  e.dma_start(out=vec1s[0:108, :], in_=vec1_d[:, :]),
                                  e.dma_start(out=ccs[0:24, :], in_=cc_d[:, :]),
                                  e.dma_start(out=lamb, in_=lam_d[:, :]),
                                  e.dma_start(out=gs_bc[:, :], in_=gsub_d[:, :]),
                                  e.dma_start(out=edge[:, :, :].rearrange("p g t -> p (g t)"), in_=edge_d[:, :])],
                 w=["consts", ("scr", 1), ("scr", 2), ("scr", 3), ("scr", 4)], dma="consts", ninst=7)
            S.op("dve", lambda e: e.tensor_copy(out=ident_b[:, :], in_=ident_f[:, :]), r=["consts"], w=["ident_b"])
            S.op("dve", lambda e: e.tensor_copy(out=perm_b[:, :], in_=perm_f), r=["consts", ("scr", 1)], w=["perm_b"])
            S.op("dve", lambda e: e.memset(ones_f[:, :], 1.0), w=["ones_f"])
            S.op("dve", lambda e: e.memset(chalf[:, :], -0.5), w=["chalf"])
            S.op("dve", lambda e: e.memset(Vb[:, :, :], 0.0), w=["V"])
            vones = Vb[:, :, :].rearrange("p k (h e) -> p k h e", e=130)[:, :, :, 128:129]
            S.op("dve", lambda e: e.memset(vones, 1.0), w=["V"])
            S.op("dve", lambda e: e.memset(uhalo[:, :, :, :], 0.0), w=["uhalo"])
            S.op("pe", lambda e: e.transpose(ps[:, 7, 0:108], vec1s[0:108, :], ident_f[0:108, 0:108]),
                 r=["consts", ("scr", 2)], w=[psk(7)])
            S.op("dve", lambda e: e.tensor_copy(out=vecT[:, :], in_=ps[:, 7, 0:108]), r=[psk(7)], w=["vecT"])
            S.op("pe", lambda e: e.transpose(ps[:, 7, 0:24], ccs[0:24, :], ident_f[0:24, 0:24]),
                 r=["consts", ("scr", 3)], w=[psk(7)])
            S.op("dve", lambda e: e.tensor_copy(out=ccT[:, :], in_=ps[:, 7, 0:24]), r=[psk(7)], w=["ccT"])
            S.op("act", lambda e: e.activation(out=scT[:, :], in_=ccT[:, :], func=AF.Silu), r=["ccT"], w=["scT"])
            scv = scT[:, :].rearrange("p (j k) -> p j k", j=3)

            gnv = vecT[:, 72:96].rearrange("p (a c) -> p a c", a=3)

            def mods_part(k0, k1, a0, a1):
                for k in range(k0, k1):
                    slot = ws.acquire(("mod", k))
                    w8 = sl8(slot)

                    def mm(e, k=k, w8=w8):
                        last = None
                        for nn in range(4):
                            n = k * 4 + nn
                            for kc in range(8):
                                last = e.matmul(ps[:, 6, n * 3:(n + 1) * 3], lhsT=w8[:, kc, nn * 128:(nn + 1) * 128],
                                                rhs=scv[:, :, kc], start=(kc == 0), stop=(kc == 7))
                        return last
                    S.op("pe", mm, r=[("ring", slot), "scT"], w=[psk(6)])
                    ws.release()
                n0, n1 = k0 * 4, k1 * 4
                psm = ps[:, 6, n0 * 3:n1 * 3].rearrange("p (n j) -> p n j", j=3)
                S.op("dve", lambda e: e.tensor_tensor(out=modT[:, n0:n1, :], in0=psm,
                                                      in1=vecT[:, n0:n1].unsqueeze(2).broadcast_to([128, n1 - n0, 3]), op=ALU.add),
                     r=[psk(6), "vecT"], w=["modT"])
                na = a1 - a0
                S.op("dve", lambda e: e.tensor_scalar_add(out=AA[:, a0:a1, :, :], in0=modv[:, a0:a1, 1, :, :], scalar1=1.0),
                     r=["modT"], w=["AA"])
                S.op("dve", lambda e: e.tensor_tensor(out=AA[:, a0:a1, :, :], in0=AA[:, a0:a1, :, :],
                                                      in1=gnv[:, a0:a1, :].unsqueeze(3).broadcast_to([128, na, 8, 3]), op=ALU.mult),
                     r=["AA", "vecT"], w=["AA"])
                S.op("dve", lambda e: e.tensor_scalar_mul(out=GG[:, a0:a1, :, :], in0=modv[:, a0:a1, 2, :, :], scalar1=0.5),
                     r=["modT"], w=["GG"])

            mods_part(0, 6, 0, 1)
            lv = lamb.rearrange("p (a d) -> p a d", a=4)
            S.op("dve", lambda e: e.tensor_tensor(out=scr[:, 0, 0:64], in0=lv[:, 0, :], in1=lv[:, 1, :], op=ALU.mult),
                 r=["consts", ("scr", 4)], w=[("scr", 0)])
            S.op("dve", lambda e: e.tensor_tensor(out=scr[:, 0, 64:128], in0=lv[:, 2, :], in1=lv[:, 3, :], op=ALU.mult),
                 r=["consts", ("scr", 4), ("scr", 0)], w=[("scr", 0)])
            S.op("dve", lambda e: e.tensor_reduce(out=lsm[:, 0:2], in_=scr[:, 0, 0:128].rearrange("p (a d) -> p a d", a=2),
                                                  axis=AX.X, op=ALU.add), r=[("scr", 0)], w=["lsm"])
            S.op("act", lambda e: e.activation(out=lsm[:, 2:4], in_=lsm[:, 0:2], func=AF.Exp), r=["lsm"], w=["lsm"])
            S.op("dve", lambda e: e.tensor_tensor(out=lsm[:, 4:5], in0=lsm[:, 3:4], in1=lsm[:, 2:3], op=ALU.subtract),
                 r=["lsm"], w=["lsm"])
            S.op("dve", lambda e: e.tensor_scalar_add(out=lsm[:, 5:6], in0=lsm[:, 4:5], scalar1=-LAM_INIT),
                 r=["lsm"], w=["lsm"])
            nlam = lsm[:, 5:6]
            S.op("dve", lambda e: e.tensor_scalar_mul(out=gs_bc[:, :], in0=gs_bc[:, :], scalar1=1.0 - LAM_INIT),
                 r=["consts"], w=["gs_bc"])

            def xk(xs, c):
                return ("x", xs, c)

            def stat_sq(xs, n, c, eng="act"):
                s_ = sc.get()
                if eng == "act":
                    S.op("act", lambda e: e.activation(out=scr[:, s_, 0:n], in_=xT[:, xs, c, 0:n], func=AF.Square),
                         r=[xk(xs, c)], w=[("scr", s_)])
                else:
                    S.op("dve", lambda e: e.tensor_tensor(out=scr[:, s_, 0:n], in0=xT[:, xs, c, 0:n], in1=xT[:, xs, c, 0:n],
                                                          op=ALU.mult), r=[xk(xs, c)], w=[("scr", s_)])
                return s_

            def stat_mm(n, c, s_):
                S.op("pe", lambda e: e.matmul(ps[:, 6, 0:n], lhsT=ones_f[:, :], rhs=scr[:, s_, 0:n],
                                              start=(c == 0), stop=(c == 7)),
                     r=[("scr", s_), "ones_f"], w=[psk(6)])

            def stat_fin(n):
                S.op("dve", lambda e: e.tensor_scalar(out=rstd[:, 0:n], in0=ps[:, 6, 0:n], scalar1=1.0 / D, scalar2=EPS,
                                                      op0=ALU.mult, op1=ALU.add), r=[psk(6)], w=["rstd"])
                if POOL_POW:
                    S.op("pool", lambda e: e.tensor_tensor(out=rstd[:, 0:n], in0=rstd[:, 0:n],
                                                           in1=chalf[:, 0:1].broadcast_to([128, n]), op=ALU.pow),
                         r=["rstd", "chalf"], w=["rstd"])
                elif LN_EXP:
                    S.op("act", lambda e: e.activation(out=rstd[:, 0:n], in_=rstd[:, 0:n], func=AF.Ln), r=["rstd"], w=["rstd"])
                    S.op("act", lambda e: e.activation(out=rstd[:, 0:n], in_=rstd[:, 0:n], func=AF.Exp, scale=-0.5),
                         r=["rstd"], w=["rstd"])
                else:
                    S.op("act", lambda e: e.activation(out=rstd[:, 0:n], in_=rstd[:, 0:n], func=AF.Sqrt), r=["rstd"], w=["rstd"])
                    S.op("dve", lambda e: e.reciprocal(out=rstd[:, 0:n], in_=rstd[:, 0:n]), r=["rstd"], w=["rstd"])

            def norm_stats(xs, n):
                for c in range(8):
                    s_ = stat_sq(xs, n, c, "act" if c % 2 == 0 else "dve")
                    stat_mm(n, c, s_)
                stat_fin(n)

            def norm_mod(xs, n, m, j, have_stats=False):
                if not have_stats:
                    norm_stats(xs, n)
                for c in range(8):
                    s = sc.get()
                    S.op("dve", lambda e, c=c, s=s: e.tensor_tensor(out=scr[:, s, 0:n], in0=xT[:, xs, c, 0:n],
                                                                    in1=rstd[:, 0:n], op=ALU.mult),
                         r=[xk(xs, c), "rstd"], w=[("scr", s)])
                    S.op("act", lambda e, c=c, s=s: e.activation(out=hT[:, c, 0:n], in_=scr[:, s, 0:n], func=AF.Identity,
                                                                 bias=modv[:, m, 0, c, j:j + 1], scale=AA[:, m, c, j:j + 1]),
                         r=[("scr", s), "AA", "modT"], w=[("hT", c)])

            hT_all = [("hT", c) for c in range(8)]
            ATK = [("R", f) for f in range(8, 16)]

            def ffn(xs, n, l, j, m, have_stats=False, stats_after=False):
                norm_mod(xs, n, m, j, have_stats)
                for i in range(11):
                    slot = ws.acquire(("gu", l, i))
                    w8 = sl8(slot)
                    for jj in range(2):
                        fj = 2 * i + jj
                        pa, pb = pr6.get(), pr6.get()

                        def mma(e, w8=w8, jj=jj, pa=pa):
                            last = None
                            for kc in range(8):
                                last = e.matmul(ps[:, pa, 0:n], lhsT=w8[:, kc, jj * 128:(jj + 1) * 128], rhs=hT[:, kc, 0:n],
                                                start=(kc == 0), stop=(kc == 7))
                            return last

                        def mmb(e, w8=w8, jj=jj, pb=pb):
                            last = None
                            for kc in range(8):
                                last = e.matmul(ps[:, pb, 0:n], lhsT=w8[:, kc, 256 + jj * 128:256 + (jj + 1) * 128],
                                                rhs=hT[:, kc, 0:n], start=(kc == 0), stop=(kc == 7))
                            return last
                        if i == 0 and jj == 0:
                            for kc in range(8):
                                S.op("pe", lambda e, kc=kc, w8=w8, pa=pa: e.matmul(ps[:, pa, 0:n], lhsT=w8[:, kc, 0:128], rhs=hT[:, kc, 0:n],
                                                                                   start=(kc == 0), stop=(kc == 7)),
                                     r=[("ring", slot), ("hT", kc)], w=[psk(pa)])
                        else:
                            S.op("pe", mma, r=[("ring", slot)] + hT_all, w=[psk(pa)])
                        S.op("pe", mmb, r=[("ring", slot)] + hT_all, w=[psk(pb)])
                        s = sc.get()
                        S.op("act", lambda e, s=s, pa=pa: e.activation(out=scr[:, s, 0:n], in_=ps[:, pa, 0:n], func=AF.Silu),
                             r=[psk(pa)], w=[("scr", s)])
                        S.op("dve", lambda e, s=s, pb=pb, fj=fj: e.tensor_tensor(out=actT[:, fj, 0:n], in0=scr[:, s, 0:n],
                                                                                 in1=ps[:, pb, 0:n], op=ALU.mult),
                             r=[("scr", s), psk(pb)], w=[("R", fj)])
                    ws.release()
                act_all = [("R", f) for f in range(NFC)]
                for d in range(8):
                    slot = ws.acquire(("down", l, d))
                    wv = ring[:, slot, 0:NFC * 128].rearrange("p (fc n) -> p fc n", fc=NFC)
                    po = pr6.get()

                    def mmd(e, wv=wv, po=po):
                        last = None
                        for fc in range(NFC):
                            last = e.matmul(ps[:, po, 0:n], lhsT=wv[:, fc, :], rhs=actT[:, fc, 0:n],
                                            start=(fc == 0), stop=(fc == NFC - 1))
                        return last
                    S.op("pe", mmd, r=[("ring", slot)] + act_all, w=[psk(po)])
                    if stats_after and d > 0:
                        stat_mm(n, d - 1, pend)
                    S.op("dve", lambda e, d=d, po=po: e.scalar_tensor_tensor(out=xT[:, xs, d, 0:n], in0=ps[:, po, 0:n],
                                                                            scalar=GG[:, m, d, j:j + 1], in1=xT[:, xs, d, 0:n],
                                                                            op0=ALU.mult, op1=ALU.add),
                         r=[psk(po), "GG", xk(xs, d)], w=[xk(xs, d)])
                    if stats_after:
                        pend = stat_sq(xs, n, d, "act")
                    ws.release()
                if stats_after:
                    stat_mm(n, 7, pend)
                    stat_fin(n)

            def rope_to(pk, n, dest, dkeys):
                s0, s1, s2 = sc.get(), sc.get(), sc.get()
                psw = pr6.get()
                S.op("act", lambda e: e.activation(out=scrb[:, s0, 0:n], in_=ps[:, pk, 0:n], func=AF.Copy),
                     r=[psk(pk)], w=[("scr", s0)])
                S.op("pe", lambda e: e.matmul(ps[:, psw, 0:n], lhsT=perm_b[:, :], rhs=scrb[:, s0, 0:n], start=True, stop=True),
                     r=[("scr", s0), "perm_b"], w=[psk(psw)])
                S.op("dve", lambda e: e.tensor_tensor(out=scr[:, s1, 0:n], in0=ps[:, pk, 0:n], in1=rope[:, 0, 0:n], op=ALU.mult),
                     r=[psk(pk), "rope"], w=[("scr", s1)])
                S.op("dve", lambda e: e.tensor_tensor(out=scr[:, s2, 0:n], in0=ps[:, psw, 0:n], in1=rope[:, 1, 0:n], op=ALU.mult),
                     r=[psk(psw), "rope"], w=[("scr", s2)])
                S.op("dve", lambda e: e.tensor_tensor(out=dest, in0=scr[:, s1, 0:n], in1=scr[:, s2, 0:n], op=ALU.add),
                     r=[("scr", s1), ("scr", s2)], w=dkeys)

            def proj_fm(slot, col, n, pk, split=False):
                w8 = sl8(slot)
                if split:
                    for kc in range(8):
                        S.op("pe", lambda e, kc=kc: e.matmul(ps[:, pk, 0:n], lhsT=w8[:, kc, col:col + 128], rhs=hT[:, kc, 0:n],
                                                             start=(kc == 0), stop=(kc == 7)),
                             r=[("ring", slot), ("hT", kc)], w=[psk(pk)])
                    return

                def mm(e):
                    last = None
                    for kc in range(8):
                        last = e.matmul(ps[:, pk, 0:n], lhsT=w8[:, kc, col:col + 128], rhs=hT[:, kc, 0:n],
                                        start=(kc == 0), stop=(kc == 7))
                    return last
                S.op("pe", mm, r=[("ring", slot)] + hT_all, w=[psk(pk)])

            def load_tile(src_d, b, r0, n, xs):
                for sub in range(n // 128):
                    st = alt.get()
                    S.op("sp", lambda e, st=st, sub=sub: e.dma_start(out=stg[:, st, :],
                                                                     in_=src_d[b, r0 + sub * 128:r0 + (sub + 1) * 128, :]),
                         w=[("stg", st)], dma=("stgin", st))
                    for cg in range(2):
                        pt = pr6.get()

                        def tr(e, st=st, cg=cg, pt=pt):
                            last = None
                            for cc in range(4):
                                c = cg * 4 + cc
                                last = e.transpose(ps[:, pt, cc * 128:(cc + 1) * 128], stg[:, st, c * 128:(c + 1) * 128],
                                                   ident_f[:, :])
                            return last
                        S.op("pe", tr, r=[("stg", st), "consts"], w=[psk(pt)])
                        eng = "dve" if cg == 0 else "act"

                        def ev(e, sub=sub, cg=cg, pt=pt, eng=eng):
                            o = xT[:, xs, cg * 4:(cg + 1) * 4, sub * 128:(sub + 1) * 128]
                            i = ps[:, pt, :].rearrange("p (c t) -> p c t", c=4)
                            if eng == "dve":
                                return e.tensor_copy(out=o, in_=i)
                            return e.activation(out=o, in_=i, func=AF.Copy)
                        S.op(eng, ev, r=[psk(pt)], w=[xk(xs, cg * 4 + cc) for cc in range(4)])

            def store_tile(b, r0, xs):
                for sub in range(4):
                    st = alt.get()
                    for cg in range(2):
                        pt = pr6.get()

                        def tr(e, sub=sub, cg=cg, pt=pt):
                            last = None
                            for cc in range(4):
                                c = cg * 4 + cc
                                last = e.transpose(ps[:, pt, cc * 128:(cc + 1) * 128], xT[:, xs, c, sub * 128:(sub + 1) * 128],
                                                   ident_f[:, :])
                            return last
                        S.op("pe", tr, r=[xk(xs, cg * 4 + cc) for cc in range(4)] + ["consts"], w=[psk(pt)])
                        eng = "dve" if cg == 0 else "act"

                        def ev(e, st=st, cg=cg, pt=pt, eng=eng):
                            o = stg[:, st, cg * 512:(cg + 1) * 512]
                            if eng == "dve":
                                return e.tensor_copy(out=o, in_=ps[:, pt, :])
                            return e.activation(out=o, in_=ps[:, pt, :], func=AF.Copy)
                        S.op(eng, ev, r=[psk(pt)], w=[("stg", st)])
                    S.op("sp", lambda e, st=st, sub=sub: e.dma_start(out=out_d[b, r0 + sub * 128:r0 + (sub + 1) * 128, :],
                                                                     in_=stg[:, st, :]),
                         r=[("stg", st)], dma=("stgout", st))

            def kv_proj(n, koff, is_lat):
                for half in range(2):
                    slot = ws.acquire(("in", K_OFF + half * 512))
                    for hh in range(4):
                        h = half * 4 + hh
                        pk = pr6.get()
                        proj_fm(slot, hh * 128, n, pk, split=(h == 0))
                        dest = KT[:, h, koff:koff + n]
                        if is_lat:
                            rope_to(pk, n, dest, [("KT", h)])
                        else:
                            S.op("act", lambda e, pk=pk, dest=dest: e.activation(out=dest, in_=ps[:, pk, 0:n], func=AF.Copy),
                                 r=[psk(pk)], w=[("KT", h)])
                    ws.release()
                for half in range(2):
                    slot = ws.acquire(("in", V_OFF + half * 512))
                    w8 = sl8(slot)
                    for sub in range(n // 128):
                        pv = pr6.get()
                        kchunk = (koff + sub * 128) // 128

                        def mm(e, w8=w8, sub=sub, pv=pv):
                            last = None
                            for kc in range(8):
                                last = e.matmul(ps[:, pv, :], lhsT=hT[:, kc, sub * 128:(sub + 1) * 128], rhs=w8[:, kc, :],
                                                start=(kc == 0), stop=(kc == 7))
                            return last
                        S.op("pe", mm, r=[("ring", slot)] + hT_all, w=[psk(pv)])
                        eng = "dve" if sub % 2 == 0 else "act"

                        def ev(e, pv=pv, kchunk=kchunk, half=half, eng=eng):
                            o = Vb[:, kchunk, half * 520:(half + 1) * 520].rearrange("p (h e) -> p h e", e=130)[:, :, 0:128]
                            i = ps[:, pv, :].rearrange("p (h e) -> p h e", e=128)
                            if eng == "dve":
                                return e.tensor_copy(out=o, in_=i)
                            return e.activation(out=o, in_=i, func=AF.Copy)
                        S.op(eng, ev, r=[psk(pv)], w=["V"])
                    ws.release()

            def u_halo(i):
                slot = ws.acquire(("in", P_OFF))
                w8 = sl8(slot)
                pu = pr6.get()

                def mm(e):
                    last = None
                    for g in range(4):
                        for side in range(2):
                            t0 = 0 if side == 0 else TT - 8
                            for kc in range(8):
                                last = e.matmul(ps[:, pu, g * 16 + side * 8:g * 16 + side * 8 + 8],
                                                lhsT=w8[:, kc, g * 128:(g + 1) * 128], rhs=hT[:, kc, t0:t0 + 8],
                                                start=(kc == 0), stop=(kc == 7))
                    return last
                S.op("pe", mm, r=[("ring", slot)] + hT_all, w=[psk(pu)])
                S.op("dve", lambda e: e.tensor_copy(out=uhalo[:, i, :, :], in_=ps[:, pu, 0:64].rearrange("p (g t) -> p g t", g=4)),
                     r=[psk(pu)], w=["uhalo"])
                ws.release()

            tilec = [0]
            mods_done = [False]

            def phase_a_tile(b, is_lat, i):
                xs = tilec[0] % 2
                tilec[0] += 1
                n = TT if is_lat else CTX
                j = b if is_lat else 2
                if is_lat:
                    load_tile(x_d, b, i * TT, n, xs)
                    if i == 3:
                        prefetch_x(0)
                    if stage >= 3:
                        S.op("sp", lambda e: [e.dma_start(out=rope[:, 0, :], in_=ropec_d[:, i * TT:(i + 1) * TT]),
                                              e.dma_start(out=rope[:, 1, :], in_=ropes_d[:, i * TT:(i + 1) * TT])],
                             w=["rope"], dma="rope", ninst=2)
                else:
                    load_tile(ctx_d, b, 0, n, xs)
                if stage >= 2:
                    ffn(xs, n, 0, j, 0, stats_after=(stage >= 3))
                    if not mods_done[0]:
                        mods_done[0] = True
                        mods_part(6, 12, 1, 2)
                if stage >= 3:
                    norm_mod(xs, n, 1, j, have_stats=True)
                    kv_proj(n, (CTX + i * TT) if is_lat else 0, is_lat)
                    if is_lat:
                        u_halo(i)
                if is_lat:
                    S.op("sp", lambda e: e.dma_start(out=xsp_d[i], in_=xT[:, xs, :, :].rearrange("p c t -> p (c t)")),
                         r=[xk(xs, c) for c in range(8)], w=[("xsp", i)], dma=("spill", i))

            def subln_scale(h0, h1):
                nh = h1 - h0
                S.op("dve", lambda e: e.tensor_scalar(out=rr[:, :, h0:h1], in0=ssq[:, :, h0:h1],
                                                      scalar1=1.0 / 128, scalar2=EPS, op0=ALU.mult, op1=ALU.add),
                     r=["ssq"], w=[("rr", h0)])
                if POOL_POW:
                    S.op("pool", lambda e: e.tensor_tensor(out=rr[:, :, h0:h1], in0=rr[:, :, h0:h1],
                                                           in1=chalf[:, 0:1].unsqueeze(2).broadcast_to([128, 4, nh]), op=ALU.pow),
                         r=[("rr", h0), "chalf"], w=[("rr", h0)])
                elif LN_EXP:
                    S.op("act", lambda e: e.activation(out=rr[:, :, h0:h1], in_=rr[:, :, h0:h1], func=AF.Ln),
                         r=[("rr", h0)], w=[("rr", h0)])
                    S.op("act", lambda e: e.activation(out=rr[:, :, h0:h1], in_=rr[:, :, h0:h1], func=AF.Exp, scale=-0.5),
                         r=[("rr", h0)], w=[("rr", h0)])
                else:
                    S.op("act", lambda e: e.activation(out=rr[:, :, h0:h1], in_=rr[:, :, h0:h1], func=AF.Sqrt),
                         r=[("rr", h0)], w=[("rr", h0)])
                    S.op("dve", lambda e: e.reciprocal(out=rr[:, :, h0:h1], in_=rr[:, :, h0:h1]), r=[("rr", h0)], w=[("rr", h0)])
                av = attn_tok[:, :, h0:h1, :]
                S.op("dve", lambda e: e.tensor_tensor(out=av, in0=av, in1=rr[:, :, h0:h1].unsqueeze(3).broadcast_to([128, 4, nh, 128]),
                                                      op=ALU.mult), r=ATK + [("rr", h0)], w=ATK)
                S.op("dve", lambda e: e.tensor_tensor(out=av, in0=av,
                                                      in1=gs_bc[:, :].unsqueeze(1).unsqueeze(1).broadcast_to([128, 4, nh, 128]),
                                                      op=ALU.mult), r=ATK + ["gs_bc"], w=ATK)

            def subln_pe(h0, h1):
                for h in range(h0, h1):
                    pt = 7 if h0 == 0 else pr6.get()

                    def tr(e, h=h, pt=pt):
                        last = None
                        for qs in range(4):
                            last = e.transpose(psb[:, pt, qs * 128:(qs + 1) * 128], attn_tok[:, qs, h, :], ident_b[:, :])
                        return last
                    S.op("pe", tr, r=ATK + ["ident_b"], w=[psk(pt)])
                    if h0 == 0 or h % 2 == 0:
                        S.op("dve", lambda e, h=h, pt=pt: e.tensor_copy(out=attnT[:, h, :], in_=psb[:, pt, 0:512]),
                             r=[psk(pt)], w=[("attnT", h)])
                    else:
                        S.op("act", lambda e, h=h, pt=pt: e.activation(out=attnT[:, h, :], in_=psb[:, pt, 0:512], func=AF.Copy),
                             r=[psk(pt)], w=[("attnT", h)])

            def attention(xs, i):
                uslot, pslot = pool_begin()
                sp_rot = Rot(2)
                steps = [(h, kc) for h in range(NH) for kc in range(NKC)]
                sb_of = {}

                def emit_qk(h, kc):
                    sb0 = 2 * sp_rot.get()
                    sb_of[(h, kc)] = sb0

                    def qk(e, h=h, kc=kc, sb0=sb0):
                        e.matmul(ps[:, sb0, :], lhsT=KT[0:64, h, kc * 128:(kc + 1) * 128], rhs=qT[0:64, h, :],
                                 start=True, stop=True)
                        return e.matmul(ps[:, sb0 + 1, :], lhsT=KT[64:128, h, kc * 128:(kc + 1) * 128],
                                        rhs=qT[64:128, h, :], start=True, stop=True)
                    S.op("pe", qk, r=[("KT", h), ("R", h)], w=[psk(sb0), psk(sb0 + 1)])

                emit_qk(*steps[0])
                for si, (h, kc) in enumerate(steps):
                    if si + 1 < len(steps):
                        emit_qk(*steps[si + 1])
                    sb0 = sb_of[(h, kc)]
                    esl = kc % 2
                    if MERGE_EXP:
                        S.op("act", lambda e, sb0=sb0, esl=esl: e.activation(out=Ebuf[:, esl, :, :], in_=ps[:, sb0:sb0 + 2, :],
                                                                             func=AF.Exp, scale=0.125),
                             r=[psk(sb0), psk(sb0 + 1)], w=[("R", 16 + 2 * esl), ("R", 17 + 2 * esl)])
                    else:
                        for c in range(2):
                            S.op("act", lambda e, sb0=sb0, esl=esl, c=c: e.activation(out=Ebuf[:, esl, c, :], in_=ps[:, sb0 + c, :],
                                                                                      func=AF.Exp, scale=0.125),
                                 r=[psk(sb0 + c)], w=[("R", 16 + 2 * esl + c)])

                    def pv(e, h=h, kc=kc, esl=esl):
                        last = None
                        for qs in range(4):
                            for c in range(2):
                                a = qs * 2 + c
                                bank, off = 4 + a // 3, (a % 3) * 130
                                last = e.matmul(ps[:, bank, off:off + 130], lhsT=Ebuf[:, esl, c, qs * 128:(qs + 1) * 128],
                                                rhs=Vb[:, kc, h * 130:(h + 1) * 130], start=(kc == 0 and a % 3 == 0),
                                                stop=(kc == NKC - 1), skip_group_check=True)
                        return last
                    S.op("pe", pv, r=[("R", 16 + 2 * esl), ("R", 17 + 2 * esl), "V"], w=[psk(4), psk(5), psk(6)])
                    if h < 4 and kc == 2:
                        pool_a(i, h, uslot)
                    if h < 4 and kc == 12:
                        pool_b(h, pslot)
                    if h == 4 and kc == 0:
                        pool_end()
                    if h == 4 and kc == 6:
                        subln_pe(0, 4)
                    if kc != NKC - 1:
                        continue
                    S.op("dve", lambda e: e.tensor_copy(out=Oev[:, 0:3, :], in_=ps[:, 4, 0:390].rearrange("p (a e) -> p a e", a=3)),
                         r=[psk(4)], w=["Oev"])
                    S.op("dve", lambda e: e.tensor_copy(out=Oev[:, 3:6, :], in_=ps[:, 5, 0:390].rearrange("p (a e) -> p a e", a=3)),
                         r=[psk(5), "Oev"], w=["Oev"])
                    S.op("dve", lambda e: e.tensor_copy(out=Oev[:, 6:8, :], in_=ps[:, 6, 0:260].rearrange("p (a e) -> p a e", a=2)),
                         r=[psk(6), "Oev"], w=["Oev"])
                    S.op("dve", lambda e: e.reciprocal(out=rsO[:, :], in_=Oev[:, :, 128]), r=["Oev"], w=["rsO"])
                    S.op("dve", lambda e: e.tensor_tensor(out=Oev[:, :, 0:128], in0=Oev[:, :, 0:128],
                                                          in1=rsO[:, :].unsqueeze(2).broadcast_to([128, 8, 128]), op=ALU.mult),
                         r=["Oev", "rsO"], w=["Oev"])
                    Ov = Oev[:, :, :].rearrange("p (q c) e -> p q c e", c=2)
                    S.op("dve", lambda e, Ov=Ov: e.scalar_tensor_tensor(out=Ov[:, :, 0, 0:128], in0=Ov[:, :, 1, 0:128], scalar=nlam,
                                                                        in1=Ov[:, :, 0, 0:128], op0=ALU.mult, op1=ALU.add),
                         r=["Oev", "lsm"], w=["Oev"])
                    s = sc.get()
                    sv = scr[:, s, 0:512].rearrange("p (q e) -> p q e", q=4)
                    S.op("dve", lambda e, Ov=Ov, sv=sv: e.tensor_tensor(out=sv, in0=Ov[:, :, 0, 0:128], in1=Ov[:, :, 0, 0:128],
                                                                        op=ALU.mult), r=["Oev"], w=[("scr", s)])
                    S.op("dve", lambda e, sv=sv, h=h: e.tensor_reduce(out=ssq[:, :, h], in_=sv, axis=AX.X, op=ALU.add),
                         r=[("scr", s)], w=["ssq"])
                    S.op("dve", lambda e, Ov=Ov, h=h: e.tensor_copy(out=attn_tok[:, :, h, :], in_=Ov[:, :, 0, 0:128]),
                         r=["Oev"], w=ATK)
                    if h == 3:
                        subln_scale(0, 4)
                subln_scale(4, 8)
                subln_pe(4, 8)

            def pool_begin():
                uslot = ws.acquire(("in", P_OFF))
                pslot = ws.acquire(("wpool",))
                return uslot, pslot

            def pool_a(i, g, uslot):
                pu = 7
                proj_fm(uslot, g * 128, TT, pu)
                S.op("dve", lambda e: e.tensor_copy(out=ug[:, 8:520], in_=ps[:, pu, :]), r=[psk(pu)], w=["ug"])
                if i > 0:
                    S.op("dve", lambda e: e.tensor_copy(out=ug[:, 0:8], in_=uhalo[:, i - 1, g, 8:16]),
                         r=["uhalo", "ug"], w=["ug"])
                else:
                    S.op("dve", lambda e: e.memset(ug[:, 0:8], 0.0), r=["ug"], w=["ug"])
                if i < 3:
                    S.op("dve", lambda e: e.tensor_copy(out=ug[:, 520:528], in_=uhalo[:, i + 1, g, 0:8]),
                         r=["uhalo", "ug"], w=["ug"])
                else:
                    S.op("dve", lambda e: e.memset(ug[:, 520:528], 0.0), r=["ug"], w=["ug"])
                w = 2 ** (g + 1)
                lo, hi, sh = 1, 528, 1
                prev = sc.get()
                S.op("dve", lambda e, s=prev: e.tensor_tensor(out=scr[:, s, 1:528], in0=ug[:, 0:527], in1=ug[:, 1:528], op=ALU.add),
                     r=["ug"], w=[("scr", prev)])
                step = 2
                while step < w:
                    s2 = sc.get()
                    nlo, nhi = lo + sh, hi - sh

                    def dbl(e, s2=s2, p=prev, nlo=nlo, nhi=nhi, sh=sh):
                        return e.tensor_tensor(out=scr[:, s2, nlo:nhi], in0=scr[:, p, nlo - sh:nhi - sh],
                                               in1=scr[:, p, nlo + sh:nhi + sh], op=ALU.add)
                    S.op("dve", dbl, r=[("scr", prev)], w=[("scr", s2)])
                    lo, hi, prev = nlo, nhi, s2
                    step *= 2
                    sh *= 2
                assert lo <= 8 and hi >= 520, (lo, hi)
                pk_ = ("pbuf", g % 2)
                pbv = pbuf[:, g % 2, :]
                S.op("dve", lambda e, p=prev: e.scalar_tensor_tensor(out=pbv, in0=scr[:, p, 8:520], scalar=1.0 / w, in1=ug[:, 8:520],
                                                                   op0=ALU.mult, op1=ALU.subtract),
                     r=[("scr", prev), "ug"], w=[pk_])
                for side, cond in ((0, i == 0), (1, i == 3)):
                    if not cond:
                        continue
                    a0 = 8 if side == 0 else 512
                    o0 = 0 if side == 0 else 504
                    s3 = sc.get()
                    S.op("dve", lambda e, p=prev, s3=s3, a0=a0, side=side: e.tensor_tensor(
                        out=scr[:, s3, 0:8], in0=scr[:, p, a0:a0 + 8], in1=edge[:, g, side * 8:side * 8 + 8], op=ALU.mult),
                        r=[("scr", prev), "consts"], w=[("scr", s3)])
                    S.op("dve", lambda e, s3=s3, a0=a0, o0=o0: e.tensor_tensor(
                        out=pbuf[:, g % 2, o0:o0 + 8], in0=scr[:, s3, 0:8], in1=ug[:, a0:a0 + 8], op=ALU.subtract),
                        r=[("scr", s3), "ug", pk_], w=[pk_])

            def pool_b(g, pslot):
                wp = ring[:, pslot, 0:512].rearrange("p (g n) -> p g n", g=4)
                pp = 7
                S.op("pe", lambda e: e.matmul(ps[:, pp, :], lhsT=wp[:, g, :], rhs=pbuf[:, g % 2, :], start=True, stop=True),
                     r=[("ring", pslot), ("pbuf", g % 2)], w=[psk(pp)])
                S.op("dve", lambda e: e.tensor_scalar_mul(out=poolT[:, g, :], in0=ps[:, pp, :], scalar1=vecT[:, 104 + g:105 + g]),
                     r=[psk(pp), "vecT"], w=["rope"])

            def pool_end():
                ws.release()
                ws.release()

            def merge(xs, j, stats_after=False):
                attnT_all = [("attnT", h) for h in range(8)]
                poolT_all = ["rope"]
                for pr in range(4):
                    gslot = ws.acquire(("G", pr))
                    bslot = ws.acquire(("B", pr))
                    g8 = sl8(gslot)
                    ba8 = ring[:, bslot, 0:2048].rearrange("p (kc n) -> p kc n", kc=8)
                    bp4 = ring[:, bslot, 2048:3072].rearrange("p (kc n) -> p kc n", kc=4)
                    for dd in range(2):
                        d = pr * 2 + dd
                        pga, pgp, pza, pzp = pr6.get(), pr6.get(), pr6.get(), pr6.get()

                        def mm8(e, w, col, rhs, pk, nk=8):
                            last = None
                            for kc in range(nk):
                                last = e.matmul(ps[:, pk, :], lhsT=w[:, kc, col:col + 128], rhs=rhs[:, kc, :],
                                                start=(kc == 0), stop=(kc == nk - 1))
                            return last
                        S.op("pe", lambda e, dd=dd, pga=pga, g8=g8: mm8(e, g8, dd * 128, hT, pga),
                             r=[("ring", gslot)] + hT_all, w=[psk(pga)])
                        S.op("pe", lambda e, dd=dd, pgp=pgp, g8=g8: mm8(e, g8, 256 + dd * 128, hT, pgp),
                             r=[("ring", gslot)] + hT_all, w=[psk(pgp)])
                        sa, sp_, s1, s2 = sc.get(), sc.get(), sc.get(), sc.get()
                        S.op("act", lambda e, pga=pga, sa=sa: e.activation(out=scr[:, sa, 0:512], in_=ps[:, pga, :], func=AF.Tanh,
                                                                          scale=0.5), r=[psk(pga)], w=[("scr", sa)])
                        S.op("act", lambda e, pgp=pgp, sp_=sp_: e.activation(out=scr[:, sp_, 0:512], in_=ps[:, pgp, :], func=AF.Tanh,
                                                                            scale=0.5), r=[psk(pgp)], w=[("scr", sp_)])
                        S.op("pe", lambda e, dd=dd, pza=pza, ba8=ba8: mm8(e, ba8, dd * 128, attnT, pza),
                             r=[("ring", bslot)] + attnT_all, w=[psk(pza)])
                        S.op("pe", lambda e, dd=dd, pzp=pzp, bp4=bp4: mm8(e, bp4, dd * 128, poolT, pzp, 4),
                             r=[("ring", bslot)] + poolT_all, w=[psk(pzp)])
                        S.op("dve", lambda e, sa=sa, pza=pza, s1=s1: e.scalar_tensor_tensor(
                            out=scr[:, s1, 0:512], in0=scr[:, sa, 0:512], scalar=1.0, in1=ps[:, pza, :], op0=ALU.add, op1=ALU.mult),
                            r=[("scr", sa), psk(pza)], w=[("scr", s1)])
                        S.op("dve", lambda e, sp_=sp_, pzp=pzp, s2=s2: e.scalar_tensor_tensor(
                            out=scr[:, s2, 0:512], in0=scr[:, sp_, 0:512], scalar=1.0, in1=ps[:, pzp, :], op0=ALU.add, op1=ALU.mult),
                            r=[("scr", sp_), psk(pzp)], w=[("scr", s2)])
                        S.op("dve", lambda e, d=d, s1=s1, s2=s2: e.tensor_tensor(out=yT[:, d, :], in0=scr[:, s1, 0:512],
                                                                                in1=scr[:, s2, 0:512], op=ALU.add),
                             r=[("scr", s1), ("scr", s2)], w=[("R", d)])
                    ws.release()
                    ws.release()
                yT_all = [("R", d) for d in range(8)]
                for half in range(2):
                    slot = ws.acquire(("out", half))
                    w8 = sl8(slot)
                    for dd in range(4):
                        d = half * 4 + dd
                        po = pr6.get()

                        def mmo(e, w8=w8, dd=dd, po=po):
                            last = None
                            for kc in range(8):
                                last = e.matmul(ps[:, po, :], lhsT=w8[:, kc, dd * 128:(dd + 1) * 128], rhs=yT[:, kc, :],
                                                start=(kc == 0), stop=(kc == 7))
                            return last
                        S.op("pe", mmo, r=[("ring", slot)] + yT_all, w=[psk(po)])
                        if stats_after and d > 0:
                            stat_mm(TT, d - 1, pend)
                        S.op("dve", lambda e, d=d, po=po: e.scalar_tensor_tensor(out=xT[:, xs, d, :], in0=ps[:, po, :],
                                                                                scalar=GG[:, 1, d, j:j + 1], in1=xT[:, xs, d, :],
                                                                                op0=ALU.mult, op1=ALU.add),
                             r=[psk(po), "GG", xk(xs, d)], w=[xk(xs, d)])
                        if stats_after:
                            pend = stat_sq(xs, TT, d, "act")
                    ws.release()
                if stats_after:
                    stat_mm(TT, 7, pend)
                    stat_fin(TT)

            def prefetch_x(i_next):
                xs_n = tilec[0] % 2
                S.op("sp", lambda e: e.dma_start(out=xT[:, xs_n, :, :].rearrange("p c t -> p (c t)"), in_=xsp_d[i_next]),
                     r=[("xsp", i_next)], w=[xk(xs_n, c) for c in range(8)], dma=("xload", xs_n))

            def phase_b_tile(b, i):
                xs = tilec[0] % 2
                tilec[0] += 1
                j = b
                if i < 3:
                    prefetch_x(i + 1)
                if stage >= 3.2:
                    S.op("sp", lambda e: [e.dma_start(out=rope[:, 0, :], in_=ropec_d[:, i * TT:(i + 1) * TT]),
                                          e.dma_start(out=rope[:, 1, :], in_=ropes_d[:, i * TT:(i + 1) * TT])],
                         w=["rope"], dma="rope", ninst=2)
                    norm_mod(xs, TT, 1, j)
                    for half in range(2):
                        slot = ws.acquire(("in", half * 512))
                        for hh in range(4):
                            h = half * 4 + hh
                            pk = pr6.get()
                            proj_fm(slot, hh * 128, TT, pk, split=(h == 0))
                            rope_to(pk, TT, qT[:, h, :], [("R", h)])
                        ws.release()
                    if stage >= 3.4:
                        attention(xs, i)
                    if stage >= 3.5:
                        merge(xs, j, stats_after=(stage >= 4))
                if stage >= 4:
                    ffn(xs, TT, 1, j, 2, have_stats=True, stats_after=True)
                    for c in range(8):
                        s = sc.get()
                        S.op("dve", lambda e, c=c, s=s: e.tensor_tensor(out=scr[:, s, 0:TT], in0=xT[:, xs, c, :], in1=rstd[:, :],
                                                                        op=ALU.mult), r=[xk(xs, c), "rstd"], w=[("scr", s)])
                        S.op("act", lambda e, c=c, s=s: e.activation(out=xT[:, xs, c, :], in_=scr[:, s, 0:TT], func=AF.Identity,
                                                                     scale=vecT[:, 96 + c:97 + c]),
                             r=[("scr", s), "vecT"], w=[xk(xs, c)])
                store_tile(b, i * TT, xs)

            for b in range(NB):
                if stage >= 2:
                    phase_a_tile(b, False, 0)
                for i in range(4):
                    phase_a_tile(b, True, i)
                if b == 0 and stage >= 2:
                    mods_part(12, 18, 2, 3)
                for i in range(4):
                    phase_b_tile(b, i)

        S0 = Sched()
        ws0 = WStream(S0, fill_fn)
        program(S0, ws0)
        tags = ws0.tags
        img_tags.clear()
        S = Sched()
        ws = WStream(S, fill_fn, tags)
        program(S, ws)
        assert ws.na == len(tags) and ws.nf == len(tags), (ws.na, ws.nf, len(tags))
        cnt = S.analyse()
        sems = {}
        for key in cnt:
            nm = "s_" + "_".join(str(x) for x in (key[1] if isinstance(key[1], tuple) else (key[1],)))
            sems[key] = es.enter_context(nc.semaphore(nm))
        S.emit(nc, {"pe": None, "act": None, "dve": None, "pool": None, "sp": None}, sems)
        with nc.Block() as block:
            @block.tensor
            def _(e):
                S.emit_engine("pe", e, sems)

            @block.scalar
            def _(e):
                S.emit_engine("act", e, sems)

            @block.vector
            def _(e):
                S.emit_engine("dve", e, sems)

            @block.gpsimd
            def _(e):
                S.emit_engine("pool", e, sems)

            @block.sync
            def _(e):
                S.emit_engine("sp", e, sems)
    return nc


def _consts():
    half = 16
    freqs = (10000.0 ** (-np.arange(half, dtype=np.float32) / half)).astype(np.float32)
    t = np.arange(L)
    row = (t // 64).astype(np.float32)
    col = (t % 64).astype(np.float32)
    ropec = np.zeros((128, L), np.float32)
    ropes = np.zeros((128, L), np.float32)
    perm = np.zeros((128, 128), np.float32)
    for p in range(128):
        d = p % 64
        axis = d // 32
        jj = d % 32
        pos = row if axis == 0 else col
        ang = (pos * freqs[jj % 16]).astype(np.float32)
        ropec[p] = np.cos(ang)
        if jj < 16:
            ropes[p] = -np.sin(ang)
            partner = p + 16
        else:
            ropes[p] = np.sin(ang)
            partner = p - 16
        perm[partner, p] = 1.0
    edge = np.zeros((4, 16), np.float32)
    for g, w in enumerate((2, 4, 8, 16)):
        for tt in range(8):
            lo = max(tt - w // 2, 0)
            hi = min(tt + w - w // 2, L)
            edge[g, tt] = 1.0 / (hi - lo)
            t2 = L - 8 + tt
            lo = max(t2 - w // 2, 0)
            hi = min(t2 + w - w // 2, L)
            edge[g, 8 + tt] = 1.0 / (hi - lo)
    edge_bc = np.ascontiguousarray(np.broadcast_to(edge.reshape(1, 64), (128, 64)))
    return ropec, ropes, perm, np.eye(128, dtype=np.float32), edge_bc


def make_in_maps(inp, ncores=NCORES):
    f = lambda a: np.ascontiguousarray(np.asarray(a, dtype=np.float32))
    ropec, ropes, perm, ident, edge_bc = _consts()
    vec1 = np.concatenate([f(inp["b_mod"]).reshape(72, 128), f(inp["g_norm"]).reshape(24, 128),
                           f(inp["g_final"]).reshape(8, 128), f(inp["pool_scale"]).reshape(4, 128)], axis=0)
    lam = np.concatenate([f(inp["lambda_q1"]).reshape(1, 64), f(inp["lambda_k1"]).reshape(1, 64),
                          f(inp["lambda_q2"]).reshape(1, 64), f(inp["lambda_k2"]).reshape(1, 64)], axis=1)
    lam_bc = np.ascontiguousarray(np.broadcast_to(lam, (128, 256)))
    gsub_bc = np.ascontiguousarray(np.broadcast_to(f(inp["g_subln"]).reshape(1, 128), (128, 128)))
    shared = {
        "vec1": np.ascontiguousarray(vec1), "w_mod": f(inp["w_mod"])[0], "w_gu": f(inp["w_ffn_gu"])[0],
        "w_down": f(inp["w_ffn_down"])[0], "w_in": f(inp["w_in"])[0], "w_ba": f(inp["w_branch_attn"])[0],
        "w_bp": f(inp["w_branch_pool"])[0], "w_out": f(inp["w_out"])[0], "w_pool": f(inp["w_pool"])[0],
        "lam_bc": lam_bc, "gsub_bc": gsub_bc, "ropec": ropec, "ropes": ropes, "ident": ident, "perm": perm,
        "edge": edge_bc,
    }
    x = f(inp["x"])
    ctx = f(inp["ctx"])
    c = f(inp["c"])
    c_ctx = f(inp["c_ctx"])
    maps = []
    for k in range(ncores):
        cc = np.concatenate([c[NB * k:NB * k + NB], c_ctx.reshape(1, D)], axis=0).reshape(24, 128)
        m = dict(shared)
        m["x"] = np.ascontiguousarray(x[NB * k:NB * k + NB])
        m["ctx"] = np.ascontiguousarray(ctx[NB * k:NB * k + NB])
        m["cc"] = np.ascontiguousarray(cc)
        maps.append(m)
    return maps


_NC_CACHE = {}


def run(inp, stage=4, ncores=NCORES, trace=False):
    if stage not in _NC_CACHE:
        _NC_CACHE[stage] = build_nc(stage)
    nc = _NC_CACHE[stage]
    maps = make_in_maps(inp, ncores)
    res = run_bass_kernel_spmd(nc, maps, core_ids=list(range(ncores)), trace=trace)
    out = np.concatenate([r["out"] for r in res.results], axis=0)
    return out, res


def kernel(**inputs):
    out, _ = run(inputs, stage=4, ncores=NCORES)
    return out.astype(np.float32)
```

```python
from contextlib import ExitStack
import math
import numpy as np
import concourse.bass as bass
import concourse.mybir as mybir
from concourse.bass_utils import run_bass_kernel_spmd

F32 = mybir.dt.float32
BF16 = mybir.dt.bfloat16
AF = mybir.ActivationFunctionType
ALU = mybir.AluOpType
AX = mybir.AxisListType

D = 1024
L = 2048
CTX = 256
NH = 8
DFF = 2816
NFC = 22
LK = CTX + L
NKC = LK // 128
K_OFF, V_OFF, P_OFF, G_OFF = 1024, 2048, 3072, 3584
EPS = 1e-6
LAM_INIT = 0.8 - 0.6 * math.exp(-0.3 * 0)
NSCR = 6
RING = 3
TT = 512
NCORES = 8
NB = 2
MERGE_EXP = True
POOL_POW = False
LN_EXP = True
NIMG = 56


class Op:
    __slots__ = ("eng", "fn", "r", "w", "dma", "deps", "signal", "cnt", "ninst")

    def __init__(self, eng, fn, r, w, dma):
        self.eng, self.fn, self.r, self.w, self.dma = eng, fn, r, w, dma
        self.deps = None
        self.signal = False
        self.cnt = 0
        self.ninst = 1


class Sched:
    def __init__(self):
        self.ops = []

    def op(self, eng, fn, r=(), w=(), dma=None, ninst=1):
        pr = [k for k in r if isinstance(k, tuple) and k[0] == "ps"]
        if pr:
            r = [k for k in r if k not in pr]
            w = list(w) + pr
        o = Op(eng, fn, tuple(r), tuple(w), dma)
        o.ninst = ninst
        self.ops.append(o)

    def analyse(self):
        last_w = {}
        readers = {}
        ops = self.ops
        for i, o in enumerate(ops):
            deps = set()
            for k in o.r:
                j = last_w.get(k)
                if j is not None:
                    deps.add(j)
            for k in o.w:
                j = last_w.get(k)
                if j is not None:
                    deps.add(j)
                rd = readers.get(k)
                if rd:
                    deps.update(rd.values())
            deps.discard(i)
            if o.eng == "pe":
                deps = {j for j in deps if ops[j].eng != "pe" or ops[j].dma}
            o.deps = deps
            for j in deps:
                ops[j].signal = True
            for k in o.r:
                rd = readers.setdefault(k, {})
                rd[o.dma if o.dma else o.eng] = i
            for k in o.w:
                last_w[k] = i
                readers[k] = {}
        cnt = {}
        for o in ops:
            key = ("d", o.dma) if o.dma else ("e", o.eng)
            if o.dma:
                cnt[key] = cnt.get(key, 0) + 16 * o.ninst
                o.cnt = cnt[key]
            elif o.signal:
                cnt[key] = cnt.get(key, 0) + 1
                o.cnt = cnt[key]
        self.final = cnt
        return cnt

    def emit(self, nc, engines, sems):
        ops = self.ops
        for ename, eng in engines.items():
            pass
        streams = {e: [] for e in engines}
        for i, o in enumerate(ops):
            streams[o.eng].append(i)
        self._streams = streams

    def emit_engine(self, ename, eng, sems):
        ops = self.ops
        known = {}
        for i in self._streams[ename]:
            o = ops[i]
            need = {}
            for j in o.deps:
                d = ops[j]
                key = ("d", d.dma) if d.dma else ("e", d.eng)
                if d.cnt > need.get(key, 0):
                    need[key] = d.cnt
            for key, v in need.items():
                if known.get(key, 0) >= v:
                    continue
                eng.wait_ge(sems[key], v)
                known[key] = v
            ins = o.fn(eng)
            if o.dma:
                if not isinstance(ins, (list, tuple)):
                    ins = [ins]
                assert len(ins) == o.ninst, (len(ins), o.ninst)
                for x in ins:
                    x.then_inc(sems[("d", o.dma)], 16)
            elif o.signal:
                if isinstance(ins, (list, tuple)):
                    ins = ins[-1]
                ins.then_inc(sems[("e", o.eng)], 1)
        if ename == "sp":
            for key, v in self.final.items():
                if known.get(key, 0) < v:
                    eng.wait_ge(sems[key], v)


class Rot:
    def __init__(self, n, start=0):
        self.n, self.i, self.s = n, 0, start

    def get(self):
        v = self.s + (self.i % self.n)
        self.i += 1
        return v


class WStream:
    def __init__(self, S, fill_fn, tags=None):
        self.S, self.fill_fn = S, fill_fn
        self.record = tags is None
        self.tags = [] if tags is None else tags
        self.na = 0
        self.nf = 0
        self.held = []
        if not self.record:
            for _ in range(min(RING, len(self.tags))):
                self._fill()

    def _fill(self):
        k = self.nf
        self.fill_fn(self.S, self.tags[k], k % RING)
        self.nf += 1

    def acquire(self, tag):
        k = self.na
        self.na += 1
        if self.record:
            self.tags.append(tag)
        else:
            assert self.tags[k] == tag, (k, self.tags[k], tag)
        self.held.append(k)
        return k % RING

    def release(self):
        self.held.pop(0)
        if not self.record and self.nf < len(self.tags):
            assert all(h > self.nf - RING for h in self.held) or not self.held, (self.held, self.nf)
            self._fill()


def build_nc(stage=4):
    nc = bass.Bass("TRN2", target_bir_lowering=False)

    def din(name, shape):
        return nc.dram_tensor(name, list(shape), F32, kind="ExternalInput").ap()

    x_d = din("x", [NB, L, D])
    ctx_d = din("ctx", [NB, CTX, D])
    cc_d = din("cc", [24, 128])
    vec1_d = din("vec1", [108, 128])
    wmod_d = din("w_mod", [D, 9 * D])
    wgu_d = din("w_gu", [2, D, 2 * DFF])
    wdn_d = din("w_down", [2, DFF, D])
    win_d = din("w_in", [D, 5632])
    wba_d = din("w_ba", [D, D])
    wbp_d = din("w_bp", [512, D])
    wout_d = din("w_out", [D, D])
    wpool_d = din("w_pool", [4, 128, 128])
    lam_d = din("lam_bc", [128, 256])
    gsub_d = din("gsub_bc", [128, 128])
    ropec_d = din("ropec", [128, L])
    ropes_d = din("ropes", [128, L])
    ident_d = din("ident", [128, 128])
    perm_d = din("perm", [128, 128])
    edge_d = din("edge", [128, 64])
    out_d = nc.dram_tensor("out", [NB, L, D], F32, kind="ExternalOutput").ap()
    xsp_d = nc.dram_tensor("xsp", [4, 128, 8 * TT], F32, kind="Internal").ap()
    wimg_d = nc.dram_tensor("wimg", [NIMG, 128, 4096], BF16, kind="Internal").ap()

    es = ExitStack()
    with es:
        def sb(name, shape, dt):
            return es.enter_context(nc.sbuf_tensor("sb_" + name, list(shape), dt))

        KT = sb("KT", [128, NH, LK], BF16)
        Vb = sb("Vb", [128, NKC, NH * 130], BF16)
        xT = sb("xT", [128, 2, 8, TT], F32)
        hT = sb("hT", [128, 8, TT], BF16)
        R22 = sb("R22", [128, NFC * TT], BF16)
        attnT = sb("attnT", [128, 8, TT], BF16)
        stg = sb("stg", [128, 2, D], F32)
        ring = sb("ring", [128, RING, 4096], BF16)
        rope = sb("rope", [128, 2, TT], F32)
        Oev = sb("Oev", [128, 8, 130], F32)
        scr = sb("scr", [128, NSCR, 528], F32)
        ug = sb("ug", [128, 528], F32)
        pbuf = sb("pbuf", [128, 2, TT], BF16)
        rstd = sb("rstd", [128, TT], F32)
        uhalo = sb("uhalo", [128, 4, 4, 16], F32)
        ident_f = sb("ident_f", [128, 128], F32)
        ident_b = sb("ident_b", [128, 128], BF16)
        perm_b = sb("perm_b", [128, 128], BF16)
        ones_f = sb("ones_f", [128, 128], F32)
        vecT = sb("vecT", [128, 108], F32)
        ccT = sb("ccT", [128, 24], F32)
        scT = sb("scT", [128, 24], BF16)
        modT = sb("modT", [128, 72, 3], F32)
        AA = sb("AA", [128, 3, 8, 3], F32)
        GG = sb("GG", [128, 3, 8, 3], F32)
        lsm = sb("lsm", [128, 8], F32)
        gs_bc = sb("gs_bc", [128, 128], F32)
        edge = sb("edge", [128, 4, 16], F32)
        rsO = sb("rsO", [128, 8], F32)
        chalf = sb("chalf", [128, 1], F32)
        ssq = sb("ssq", [128, 4, 8], F32)
        rr = sb("rr", [128, 4, 8], F32)
        ps = es.enter_context(nc.psum_tensor("ps", [128, 8, 512], F32))
        psb = ps.bitcast(BF16)

        actT = R22[:, :].rearrange("p (f t) -> p f t", t=TT)
        qT = R22[:, 0:8 * TT].rearrange("p (f t) -> p f t", t=TT)
        yT = qT
        attn_tok = R22[:, 8 * TT:16 * TT].rearrange("p (q h e) -> p q h e", q=4, h=8)
        Ebuf = R22[:, 16 * TT:20 * TT].rearrange("p (s c t) -> p s c t", s=2, c=2)
        scrb = scr.bitcast(BF16)
        poolT = rope.bitcast(BF16)[:, :, :].rearrange("p a (b t) -> p (a b) t", t=TT)
        perm_f = scr[:, 1, 0:128]
        vec1s = scr[:, 2, 0:128]
        ccs = scr[:, 3, 0:128]
        lamb = scr[:, 4, 0:256]
        modv = modT[:, :, :].rearrange("p (a k c) j -> p a k c j", a=3, k=3, c=8)

        def psk(b):
            return ("ps", b)

        img_tags = {}

        def fill_fn(S, tag, slot):
            kind = tag[0]
            rk = ("ring", slot)
            sl = ring[:, slot, :]
            if kind != "mod":
                if tag in img_tags:
                    k = img_tags[tag]
                    S.op("pool", lambda g: g.dma_start(out=sl, in_=wimg_d[k]), r=[("img", k)], w=[rk], dma=rk)
                    return
                fill_fp32(S, tag, slot)
                k = len(img_tags)
                img_tags[tag] = k
                S.op("sp", lambda e: e.dma_start(out=wimg_d[k], in_=sl), r=[rk], w=[("img", k)], dma=("wb", slot))
                return
            fill_fp32(S, tag, slot)

        def fill_fp32(S, tag, slot):
            kind = tag[0]
            rk = ("ring", slot)
            sl = ring[:, slot, :]
            if kind == "mod":
                k = tag[1]
                src = wmod_d[:, k * 512:(k + 1) * 512].rearrange("(kc p) n -> p kc n", p=128)
                S.op("pool", lambda g: g.dma_start(out=sl.rearrange("p (kc n) -> p kc n", kc=8), in_=src),
                     w=[rk], dma=rk)
            elif kind == "gu":
                _, l, i = tag
                sa = wgu_d[l, :, i * 256:(i + 1) * 256].rearrange("(kc p) n -> p kc n", p=128)
                sbb = wgu_d[l, :, DFF + i * 256:DFF + (i + 1) * 256].rearrange("(kc p) n -> p kc n", p=128)
                dv = sl.rearrange("p (kc n) -> p kc n", kc=8)
                S.op("pool", lambda g: [g.dma_start(out=dv[:, :, 0:256], in_=sa),
                                        g.dma_start(out=dv[:, :, 256:512], in_=sbb)], w=[rk], dma=rk, ninst=2)
            elif kind == "down":
                _, l, d = tag
                src = wdn_d[l, :, d * 128:(d + 1) * 128].rearrange("(fc p) n -> p fc n", p=128)
                S.op("pool", lambda g: g.dma_start(out=sl[:, 0:NFC * 128].rearrange("p (fc n) -> p fc n", fc=NFC), in_=src),
                     w=[rk], dma=rk)
            elif kind == "in":
                c0 = tag[1]
                src = win_d[:, c0:c0 + 512].rearrange("(kc p) n -> p kc n", p=128)
                S.op("pool", lambda g: g.dma_start(out=sl.rearrange("p (kc n) -> p kc n", kc=8), in_=src),
                     w=[rk], dma=rk)
            elif kind == "G":
                pr = tag[1]
                sa = win_d[:, G_OFF + pr * 256:G_OFF + (pr + 1) * 256].rearrange("(kc p) n -> p kc n", p=128)
                sbb = win_d[:, G_OFF + D + pr * 256:G_OFF + D + (pr + 1) * 256].rearrange("(kc p) n -> p kc n", p=128)
                dv = sl.rearrange("p (kc n) -> p kc n", kc=8)
                S.op("pool", lambda g: [g.dma_start(out=dv[:, :, 0:256], in_=sa),
                                        g.dma_start(out=dv[:, :, 256:512], in_=sbb)], w=[rk], dma=rk, ninst=2)
            elif kind == "B":
                pr = tag[1]
                sa = wba_d[:, pr * 256:(pr + 1) * 256].rearrange("(kc p) n -> p kc n", p=128)
                sbb = wbp_d[:, pr * 256:(pr + 1) * 256].rearrange("(kc p) n -> p kc n", p=128)
                S.op("pool", lambda g: [g.dma_start(out=sl[:, 0:2048].rearrange("p (kc n) -> p kc n", kc=8), in_=sa),
                                        g.dma_start(out=sl[:, 2048:3072].rearrange("p (kc n) -> p kc n", kc=4), in_=sbb)],
                     w=[rk], dma=rk, ninst=2)
            elif kind == "out":
                h = tag[1]
                src = wout_d[:, h * 512:(h + 1) * 512].rearrange("(kc p) n -> p kc n", p=128)
                S.op("pool", lambda g: g.dma_start(out=sl.rearrange("p (kc n) -> p kc n", kc=8), in_=src),
                     w=[rk], dma=rk)
            elif kind == "wpool":
                src = wpool_d[:, :, :].rearrange("g p n -> p g n")
                S.op("pool", lambda g: g.dma_start(out=sl[:, 0:512].rearrange("p (g n) -> p g n", g=4), in_=src),
                     w=[rk], dma=rk)
            else:
                raise ValueError(tag)

        def program(S, ws):
            pr6 = Rot(6)
            sc = Rot(NSCR)
            alt = Rot(2)

            def sl8(slot):
                return ring[:, slot, :].rearrange("p (kc n) -> p kc n", kc=8)

            S.op("sp", lambda e: [e.dma_start(out=ident_f[:, :], in_=ident_d[:, :]),
                                  e.dma_start(out=perm_f, in_=perm_d[:, :]),
                                  e.dma_start(out=vec1s[0:108, :], in_=vec1_d[:, :]),
                                  e.dma_start(out=ccs[0:24, :], in_=cc_d[:, :]),
                                  e.dma_start(out=lamb, in_=lam_d[:, :]),
                                  e.dma_start(out=gs_bc[:, :], in_=gsub_d[:, :]),
                                  e.dma_start(out=edge[:, :, :].rearrange("p g t -> p (g t)"), in_=edge_d[:, :])],
                 w=["consts", ("scr", 1), ("scr", 2), ("scr", 3), ("scr", 4)], dma="consts", ninst=7)
            S.op("dve", lambda e: e.tensor_copy(out=ident_b[:, :], in_=ident_f[:, :]), r=["consts"], w=["ident_b"])
            S.op("dve", lambda e: e.tensor_copy(out=perm_b[:, :], in_=perm_f), r=["consts", ("scr", 1)], w=["perm_b"])
            S.op("dve", lambda e: e.memset(ones_f[:, :], 1.0), w=["ones_f"])
            S.op("dve", lambda e: e.memset(chalf[:, :], -0.5), w=["chalf"])
            S.op("dve", lambda e: e.memset(Vb[:, :, :], 0.0), w=["V"])
            vones = Vb[:, :, :].rearrange("p k (h e) -> p k h e", e=130)[:, :, :, 128:129]
            S.op("dve", lambda e: e.memset(vones, 1.0), w=["V"])
            S.op("dve", lambda e: e.memset(uhalo[:, :, :, :], 0.0), w=["uhalo"])
            S.op("pe", lambda e: e.transpose(ps[:, 7, 0:108], vec1s[0:108, :], ident_f[0:108, 0:108]),
                 r=["consts", ("scr", 2)], w=[psk(7)])
            S.op("dve", lambda e: e.tensor_copy(out=vecT[:, :], in_=ps[:, 7, 0:108]), r=[psk(7)], w=["vecT"])
            S.op("pe", lambda e: e.transpose(ps[:, 7, 0:24], ccs[0:24, :], ident_f[0:24, 0:24]),
                 r=["consts", ("scr", 3)], w=[psk(7)])
            S.op("dve", lambda e: e.tensor_copy(out=ccT[:, :], in_=ps[:, 7, 0:24]), r=[psk(7)], w=["ccT"])
            S.op("act", lambda e: e.activation(out=scT[:, :], in_=ccT[:, :], func=AF.Silu), r=["ccT"], w=["scT"])
            scv = scT[:, :].rearrange("p (j k) -> p j k", j=3)

            gnv = vecT[:, 72:96].rearrange("p (a c) -> p a c", a=3)

            def mods_part(k0, k1, a0, a1):
                for k in range(k0, k1):
                    slot = ws.acquire(("mod", k))
                    w8 = sl8(slot)

                    def mm(e, k=k, w8=w8):
                        last = None
                        for nn in range(4):
                            n = k * 4 + nn
                            for kc in range(8):
                                last = e.matmul(ps[:, 6, n * 3:(n + 1) * 3], lhsT=w8[:, kc, nn * 128:(nn + 1) * 128],
                                                rhs=scv[:, :, kc], start=(kc == 0), stop=(kc == 7))
                        return last
                    S.op("pe", mm, r=[("ring", slot), "scT"], w=[psk(6)])
                    ws.release()
                n0, n1 = k0 * 4, k1 * 4
                psm = ps[:, 6, n0 * 3:n1 * 3].rearrange("p (n j) -> p n j", j=3)
                S.op("dve", lambda e: e.tensor_tensor(out=modT[:, n0:n1, :], in0=psm,
                                                      in1=vecT[:, n0:n1].unsqueeze(2).broadcast_to([128, n1 - n0, 3]), op=ALU.add),
                     r=[psk(6), "vecT"], w=["modT"])
                na = a1 - a0
                S.op("dve", lambda e: e.tensor_scalar_add(out=AA[:, a0:a1, :, :], in0=modv[:, a0:a1, 1, :, :], scalar1=1.0),
                     r=["modT"], w=["AA"])
                S.op("dve", lambda e: e.tensor_tensor(out=AA[:, a0:a1, :, :], in0=AA[:, a0:a1, :, :],
                                                      in1=gnv[:, a0:a1, :].unsqueeze(3).broadcast_to([128, na, 8, 3]), op=ALU.mult),
                     r=["AA", "vecT"], w=["AA"])
                S.op("dve", lambda e: e.tensor_scalar_mul(out=GG[:, a0:a1, :, :], in0=modv[:, a0:a1, 2, :, :], scalar1=0.5),
                     r=["modT"], w=["GG"])

            mods_part(0, 6, 0, 1)
            lv = lamb.rearrange("p (a d) -> p a d", a=4)
            S.op("dve", lambda e: e.tensor_tensor(out=scr[:, 0, 0:64], in0=lv[:, 0, :], in1=lv[:, 1, :], op=ALU.mult),
                 r=["consts", ("scr", 4)], w=[("scr", 0)])
            S.op("dve", lambda e: e.tensor_tensor(out=scr[:, 0, 64:128], in0=lv[:, 2, :], in1=lv[:, 3, :], op=ALU.mult),
                 r=["consts", ("scr", 4), ("scr", 0)], w=[("scr", 0)])
            S.op("dve", lambda e: e.tensor_reduce(out=lsm[:, 0:2], in_=scr[:, 0, 0:128].rearrange("p (a d) -> p a d", a=2),
                                                  axis=AX.X, op=ALU.add), r=[("scr", 0)], w=["lsm"])
            S.op("act", lambda e: e.activation(out=lsm[:, 2:4], in_=lsm[:, 0:2], func=AF.Exp), r=["lsm"], w=["lsm"])
            S.op("dve", lambda e: e.tensor_tensor(out=lsm[:, 4:5], in0=lsm[:, 3:4], in1=lsm[:, 2:3], op=ALU.subtract),
                 r=["lsm"], w=["lsm"])
            S.op("dve", lambda e: e.tensor_scalar_add(out=lsm[:, 5:6], in0=lsm[:, 4:5], scalar1=-LAM_INIT),
                 r=["lsm"], w=["lsm"])
            nlam = lsm[:, 5:6]
            S.op("dve", lambda e: e.tensor_scalar_mul(out=gs_bc[:, :], in0=gs_bc[:, :], scalar1=1.0 - LAM_INIT),
                 r=["consts"], w=["gs_bc"])

            def xk(xs, c):
                return ("x", xs, c)

            def stat_sq(xs, n, c, eng="act"):
                s_ = sc.get()
                if eng == "act":
                    S.op("act", lambda e: e.activation(out=scr[:, s_, 0:n], in_=xT[:, xs, c, 0:n], func=AF.Square),
                         r=[xk(xs, c)], w=[("scr", s_)])
                else:
                    S.op("dve", lambda e: e.tensor_tensor(out=scr[:, s_, 0:n], in0=xT[:, xs, c, 0:n], in1=xT[:, xs, c, 0:n],
                                                          op=ALU.mult), r=[xk(xs, c)], w=[("scr", s_)])
                return s_

            def stat_mm(n, c, s_):
                S.op("pe", lambda e: e.matmul(ps[:, 6, 0:n], lhsT=ones_f[:, :], rhs=scr[:, s_, 0:n],
                                              start=(c == 0), stop=(c == 7)),
                     r=[("scr", s_), "ones_f"], w=[psk(6)])

            def stat_fin(n):
                S.op("dve", lambda e: e.tensor_scalar(out=rstd[:, 0:n], in0=ps[:, 6, 0:n], scalar1=1.0 / D, scalar2=EPS,
                                                      op0=ALU.mult, op1=ALU.add), r=[psk(6)], w=["rstd"])
                if POOL_POW:
                    S.op("pool", lambda e: e.tensor_tensor(out=rstd[:, 0:n], in0=rstd[:, 0:n],
                                                           in1=chalf[:, 0:1].broadcast_to([128, n]), op=ALU.pow),
                         r=["rstd", "chalf"], w=["rstd"])
                elif LN_EXP:
                    S.op("act", lambda e: e.activation(out=rstd[:, 0:n], in_=rstd[:, 0:n], func=AF.Ln), r=["rstd"], w=["rstd"])
                    S.op("act", lambda e: e.activation(out=rstd[:, 0:n], in_=rstd[:, 0:n], func=AF.Exp, scale=-0.5),
                         r=["rstd"], w=["rstd"])
                else:
                    S.op("act", lambda e: e.activation(out=rstd[:, 0:n], in_=rstd[:, 0:n], func=AF.Sqrt), r=["rstd"], w=["rstd"])
                    S.op("dve", lambda e: e.reciprocal(out=rstd[:, 0:n], in_=rstd[:, 0:n]), r=["rstd"], w=["rstd"])

            def norm_stats(xs, n):
                for c in range(8):
                    s_ = stat_sq(xs, n, c, "act" if c % 2 == 0 else "dve")
                    stat_mm(n, c, s_)
                stat_fin(n)

            def norm_mod(xs, n, m, j, have_stats=False):
                if not have_stats:
                    norm_stats(xs, n)
                for c in range(8):
                    s = sc.get()
                    S.op("dve", lambda e, c=c, s=s: e.tensor_tensor(out=scr[:, s, 0:n], in0=xT[:, xs, c, 0:n],
                                                                    in1=rstd[:, 0:n], op=ALU.mult),
                         r=[xk(xs, c), "rstd"], w=[("scr", s)])
                    S.op("act", lambda e, c=c, s=s: e.activation(out=hT[:, c, 0:n], in_=scr[:, s, 0:n], func=AF.Identity,
                                                                 bias=modv[:, m, 0, c, j:j + 1], scale=AA[:, m, c, j:j + 1]),
                         r=[("scr", s), "AA", "modT"], w=[("hT", c)])

            hT_all = [("hT", c) for c in range(8)]
            ATK = [("R", f) for f in range(8, 16)]

            def ffn(xs, n, l, j, m, have_stats=False, stats_after=False):
                norm_mod(xs, n, m, j, have_stats)
                for i in range(11):
                    slot = ws.acquire(("gu", l, i))
                    w8 = sl8(slot)
                    for jj in range(2):
                        fj = 2 * i + jj
                        pa, pb = pr6.get(), pr6.get()

                        def mma(e, w8=w8, jj=jj, pa=pa):
                            last = None
                            for kc in range(8):
                                last = e.matmul(ps[:, pa, 0:n], lhsT=w8[:, kc, jj * 128:(jj + 1) * 128], rhs=hT[:, kc, 0:n],
                                                start=(kc == 0), stop=(kc == 7))
                            return last

                        def mmb(e, w8=w8, jj=jj, pb=pb):
                            last = None
                            for kc in range(8):
                                last = e.matmul(ps[:, pb, 0:n], lhsT=w8[:, kc, 256 + jj * 128:256 + (jj + 1) * 128],
                                                rhs=hT[:, kc, 0:n], start=(kc == 0), stop=(kc == 7))
                            return last
                        if i == 0 and jj == 0:
                            for kc in range(8):
                                S.op("pe", lambda e, kc=kc, w8=w8, pa=pa: e.matmul(ps[:, pa, 0:n], lhsT=w8[:, kc, 0:128], rhs=hT[:, kc, 0:n],
                                                                                   start=(kc == 0), stop=(kc == 7)),
                                     r=[("ring", slot), ("hT", kc)], w=[psk(pa)])
                        else:
                            S.op("pe", mma, r=[("ring", slot)] + hT_all, w=[psk(pa)])
                        S.op("pe", mmb, r=[("ring", slot)] + hT_all, w=[psk(pb)])
                        s = sc.get()
                        S.op("act", lambda e, s=s, pa=pa: e.activation(out=scr[:, s, 0:n], in_=ps[:, pa, 0:n], func=AF.Silu),
                             r=[psk(pa)], w=[("scr", s)])
                        S.op("dve", lambda e, s=s, pb=pb, fj=fj: e.tensor_tensor(out=actT[:, fj, 0:n], in0=scr[:, s, 0:n],
                                                                                 in1=ps[:, pb, 0:n], op=ALU.mult),
                             r=[("scr", s), psk(pb)], w=[("R", fj)])
                    ws.release()
                act_all = [("R", f) for f in range(NFC)]
                for d in range(8):
                    slot = ws.acquire(("down", l, d))
                    wv = ring[:, slot, 0:NFC * 128].rearrange("p (fc n) -> p fc n", fc=NFC)
                    po = pr6.get()

                    def mmd(e, wv=wv, po=po):
                        last = None
                        for fc in range(NFC):
                            last = e.matmul(ps[:, po, 0:n], lhsT=wv[:, fc, :], rhs=actT[:, fc, 0:n],
                                            start=(fc == 0), stop=(fc == NFC - 1))
                        return last
                    S.op("pe", mmd, r=[("ring", slot)] + act_all, w=[psk(po)])
                    if stats_after and d > 0:
                        stat_mm(n, d - 1, pend)
                    S.op("dve", lambda e, d=d, po=po: e.scalar_tensor_tensor(out=xT[:, xs, d, 0:n], in0=ps[:, po, 0:n],
                                                                            scalar=GG[:, m, d, j:j + 1], in1=xT[:, xs, d, 0:n],
                                                                            op0=ALU.mult, op1=ALU.add),
                         r=[psk(po), "GG", xk(xs, d)], w=[xk(xs, d)])
                    if stats_after:
                        pend = stat_sq(xs, n, d, "act")
                    ws.release()
                if stats_after:
                    stat_mm(n, 7, pend)
                    stat_fin(n)

            def rope_to(pk, n, dest, dkeys):
                s0, s1, s2 = sc.get(), sc.get(), sc.get()
                psw = pr6.get()
                S.op("act", lambda e: e.activation(out=scrb[:, s0, 0:n], in_=ps[:, pk, 0:n], func=AF.Copy),
                     r=[psk(pk)], w=[("scr", s0)])
                S.op("pe", lambda e: e.matmul(ps[:, psw, 0:n], lhsT=perm_b[:, :], rhs=scrb[:, s0, 0:n], start=True, stop=True),
                     r=[("scr", s0), "perm_b"], w=[psk(psw)])
                S.op("dve", lambda e: e.tensor_tensor(out=scr[:, s1, 0:n], in0=ps[:, pk, 0:n], in1=rope[:, 0, 0:n], op=ALU.mult),
                     r=[psk(pk), "rope"], w=[("scr", s1)])
                S.op("dve", lambda e: e.tensor_tensor(out=scr[:, s2, 0:n], in0=ps[:, psw, 0:n], in1=rope[:, 1, 0:n], op=ALU.mult),
                     r=[psk(psw), "rope"], w=[("scr", s2)])
                S.op("dve", lambda e: e.tensor_tensor(out=dest, in0=scr[:, s1, 0:n], in1=scr[:, s2, 0:n], op=ALU.add),
                     r=[("scr", s1), ("scr", s2)], w=dkeys)

            def proj_fm(slot, col, n, pk, split=False):
                w8 = sl8(slot)
                if split:
                    for kc in range(8):
                        S.op("pe", lambda e, kc=kc: e.matmul(ps[:, pk, 0:n], lhsT=w8[:, kc, col:col + 128], rhs=hT[:, kc, 0:n],
                                                             start=(kc == 0), stop=(kc == 7)),
                             r=[("ring", slot), ("hT", kc)], w=[psk(pk)])
                    return

                def mm(e):
                    last = None
                    for kc in range(8):
                        last = e.matmul(ps[:, pk, 0:n], lhsT=w8[:, kc, col:col + 128], rhs=hT[:, kc, 0:n],
                                        start=(kc == 0), stop=(kc == 7))
                    return last
                S.op("pe", mm, r=[("ring", slot)] + hT_all, w=[psk(pk)])

            def load_tile(src_d, b, r0, n, xs):
                for sub in range(n // 128):
                    st = alt.get()
                    S.op("sp", lambda e, st=st, sub=sub: e.dma_start(out=stg[:, st, :],
                                                                     in_=src_d[b, r0 + sub * 128:r0 + (sub + 1) * 128, :]),
                         w=[("stg", st)], dma=("stgin", st))
                    for cg in range(2):
                        pt = pr6.get()

                        def tr(e, st=st, cg=cg, pt=pt):
                            last = None
                            for cc in range(4):
                                c = cg * 4 + cc
                                last = e.transpose(ps[:, pt, cc * 128:(cc + 1) * 128], stg[:, st, c * 128:(c + 1) * 128],
                                                   ident_f[:, :])
                            return last
                        S.op("pe", tr, r=[("stg", st), "consts"], w=[psk(pt)])
                        eng = "dve" if cg == 0 else "act"

                        def ev(e, sub=sub, cg=cg, pt=pt, eng=eng):
                            o = xT[:, xs, cg * 4:(cg + 1) * 4, sub * 128:(sub + 1) * 128]
                            i = ps[:, pt, :].rearrange("p (c t) -> p c t", c=4)
                            if eng == "dve":
                                return e.tensor_copy(out=o, in_=i)
                            return e.activation(out=o, in_=i, func=AF.Copy)
                        S.op(eng, ev, r=[psk(pt)], w=[xk(xs, cg * 4 + cc) for cc in range(4)])

            def store_tile(b, r0, xs):
                for sub in range(4):
                    st = alt.get()
                    for cg in range(2):
                        pt = pr6.get()

                        def tr(e, sub=sub, cg=cg, pt=pt):
                            last = None
                            for cc in range(4):
                                c = cg * 4 + cc
                                last = e.transpose(ps[:, pt, cc * 128:(cc + 1) * 128], xT[:, xs, c, sub * 128:(sub + 1) * 128],
                                                   ident_f[:, :])
                            return last
                        S.op("pe", tr, r=[xk(xs, cg * 4 + cc) for cc in range(4)] + ["consts"], w=[psk(pt)])
                        eng = "dve" if cg == 0 else "act"

                        def ev(e, st=st, cg=cg, pt=pt, eng=eng):
                            o = stg[:, st, cg * 512:(cg + 1) * 512]
                            if eng == "dve":
                                return e.tensor_copy(out=o, in_=ps[:, pt, :])
                            return e.activation(out=o, in_=ps[:, pt, :], func=AF.Copy)
                        S.op(eng, ev, r=[psk(pt)], w=[("stg", st)])
                    S.op("sp", lambda e, st=st, sub=sub: e.dma_start(out=out_d[b, r0 + sub * 128:r0 + (sub + 1) * 128, :],
                                                                     in_=stg[:, st, :]),
                         r=[("stg", st)], dma=("stgout", st))

            def kv_proj(n, koff, is_lat):
                for half in range(2):
                    slot = ws.acquire(("in", K_OFF + half * 512))
                    for hh in range(4):
                        h = half * 4 + hh
                        pk = pr6.get()
                        proj_fm(slot, hh * 128, n, pk, split=(h == 0))
                        dest = KT[:, h, koff:koff + n]
                        if is_lat:
                            rope_to(pk, n, dest, [("KT", h)])
                        else:
                            S.op("act", lambda e, pk=pk, dest=dest: e.activation(out=dest, in_=ps[:, pk, 0:n], func=AF.Copy),
                                 r=[psk(pk)], w=[("KT", h)])
                    ws.release()
                for half in range(2):
                    slot = ws.acquire(("in", V_OFF + half * 512))
                    w8 = sl8(slot)
                    for sub in range(n // 128):
                        pv = pr6.get()
                        kchunk = (koff + sub * 128) // 128

                        def mm(e, w8=w8, sub=sub, pv=pv):
                            last = None
                            for kc in range(8):
                                last = e.matmul(ps[:, pv, :], lhsT=hT[:, kc, sub * 128:(sub + 1) * 128], rhs=w8[:, kc, :],
                                                start=(kc == 0), stop=(kc == 7))
                            return last
                        S.op("pe", mm, r=[("ring", slot)] + hT_all, w=[psk(pv)])
                        eng = "dve" if sub % 2 == 0 else "act"

                        def ev(e, pv=pv, kchunk=kchunk, half=half, eng=eng):
                            o = Vb[:, kchunk, half * 520:(half + 1) * 520].rearrange("p (h e) -> p h e", e=130)[:, :, 0:128]
                            i = ps[:, pv, :].rearrange("p (h e) -> p h e", e=128)
                            if eng == "dve":
                                return e.tensor_copy(out=o, in_=i)
                            return e.activation(out=o, in_=i, func=AF.Copy)
                        S.op(eng, ev, r=[psk(pv)], w=["V"])
                    ws.release()

            def u_halo(i):
                slot = ws.acquire(("in", P_OFF))
                w8 = sl8(slot)
                pu = pr6.get()

                def mm(e):
                    last = None
                    for g in range(4):
                        for side in range(2):
                            t0 = 0 if side == 0 else TT - 8
                            for kc in range(8):
                                last = e.matmul(ps[:, pu, g * 16 + side * 8:g * 16 + side * 8 + 8],
                                                lhsT=w8[:, kc, g * 128:(g + 1) * 128], rhs=hT[:, kc, t0:t0 + 8],
                                                start=(kc == 0), stop=(kc == 7))
                    return last
                S.op("pe", mm, r=[("ring", slot)] + hT_all, w=[psk(pu)])
                S.op("dve", lambda e: e.tensor_copy(out=uhalo[:, i, :, :], in_=ps[:, pu, 0:64].rearrange("p (g t) -> p g t", g=4)),
                     r=[psk(pu)], w=["uhalo"])
                ws.release()

            tilec = [0]
            mods_done = [False]

            def phase_a_tile(b, is_lat, i):
                xs = tilec[0] % 2
                tilec[0] += 1
                n = TT if is_lat else CTX
                j = b if is_lat else 2
                if is_lat:
                    load_tile(x_d, b, i * TT, n, xs)
                    if i == 3:
                        prefetch_x(0)
                    if stage >= 3:
                        S.op("sp", lambda e: [e.dma_start(out=rope[:, 0, :], in_=ropec_d[:, i * TT:(i + 1) * TT]),
                                              e.dma_start(out=rope[:, 1, :], in_=ropes_d[:, i * TT:(i + 1) * TT])],
                             w=["rope"], dma="rope", ninst=2)
                else:
                    load_tile(ctx_d, b, 0, n, xs)
                if stage >= 2:
                    ffn(xs, n, 0, j, 0, stats_after=(stage >= 3))
                    if not mods_done[0]:
                        mods_done[0] = True
                        mods_part(6, 12, 1, 2)
                if stage >= 3:
                    norm_mod(xs, n, 1, j, have_stats=True)
                    kv_proj(n, (CTX + i * TT) if is_lat else 0, is_lat)
                    if is_lat:
                        u_halo(i)
                if is_lat:
                    S.op("sp", lambda e: e.dma_start(out=xsp_d[i], in_=xT[:, xs, :, :].rearrange("p c t -> p (c t)")),
                         r=[xk(xs, c) for c in range(8)], w=[("xsp", i)], dma=("spill", i))

            def subln_scale(h0, h1):
                nh = h1 - h0
                S.op("dve", lambda e: e.tensor_scalar(out=rr[:, :, h0:h1], in0=ssq[:, :, h0:h1],
                                                      scalar1=1.0 / 128, scalar2=EPS, op0=ALU.mult, op1=ALU.add),
                     r=["ssq"], w=[("rr", h0)])
                if POOL_POW:
                    S.op("pool", lambda e: e.tensor_tensor(out=rr[:, :, h0:h1], in0=rr[:, :, h0:h1],
                                                           in1=chalf[:, 0:1].unsqueeze(2).broadcast_to([128, 4, nh]), op=ALU.pow),
                         r=[("rr", h0), "chalf"], w=[("rr", h0)])
                elif LN_EXP:
                    S.op("act", lambda e: e.activation(out=rr[:, :, h0:h1], in_=rr[:, :, h0:h1], func=AF.Ln),
                         r=[("rr", h0)], w=[("rr", h0)])
                    S.op("act", lambda e: e.activation(out=rr[:, :, h0:h1], in_=rr[:, :, h0:h1], func=AF.Exp, scale=-0.5),
                         r=[("rr", h0)], w=[("rr", h0)])
                else:
                    S.op("act", lambda e: e.activation(out=rr[:, :, h0:h1], in_=rr[:, :, h0:h1], func=AF.Sqrt),
                         r=[("rr", h0)], w=[("rr", h0)])
                    S.op("dve", lambda e: e.reciprocal(out=rr[:, :, h0:h1], in_=rr[:, :, h0:h1]), r=[("rr", h0)], w=[("rr", h0)])
                av = attn_tok[:, :, h0:h1, :]
                S.op("dve", lambda e: e.tensor_tensor(out=av, in0=av, in1=rr[:, :, h0:h1].unsqueeze(3).broadcast_to([128, 4, nh, 128]),
                                                      op=ALU.mult), r=ATK + [("rr", h0)], w=ATK)
                S.op("dve", lambda e: e.tensor_tensor(out=av, in0=av,
                                                      in1=gs_bc[:, :].unsqueeze(1).unsqueeze(1).broadcast_to([128, 4, nh, 128]),
                                                      op=ALU.mult), r=ATK + ["gs_bc"], w=ATK)

            def subln_pe(h0, h1):
                for h in range(h0, h1):
                    pt = 7 if h0 == 0 else pr6.get()

                    def tr(e, h=h, pt=pt):
                        last = None
                        for qs in range(4):
                            last = e.transpose(psb[:, pt, qs * 128:(qs + 1) * 128], attn_tok[:, qs, h, :], ident_b[:, :])
                        return last
                    S.op("pe", tr, r=ATK + ["ident_b"], w=[psk(pt)])
                    if h0 == 0 or h % 2 == 0:
                        S.op("dve", lambda e, h=h, pt=pt: e.tensor_copy(out=attnT[:, h, :], in_=psb[:, pt, 0:512]),
                             r=[psk(pt)], w=[("attnT", h)])
                    else:
                        S.op("act", lambda e, h=h, pt=pt: e.activation(out=attnT[:, h, :], in_=psb[:, pt, 0:512], func=AF.Copy),
                             r=[psk(pt)], w=[("attnT", h)])

            def attention(xs, i):
                uslot, pslot = pool_begin()
                sp_rot = Rot(2)
                steps = [(h, kc) for h in range(NH) for kc in range(NKC)]
                sb_of = {}

                def emit_qk(h, kc):
                    sb0 = 2 * sp_rot.get()
                    sb_of[(h, kc)] = sb0

                    def qk(e, h=h, kc=kc, sb0=sb0):
                        e.matmul(ps[:, sb0, :], lhsT=KT[0:64, h, kc * 128:(kc + 1) * 128], rhs=qT[0:64, h, :],
                                 start=True, stop=True)
                        return e.matmul(ps[:, sb0 + 1, :], lhsT=KT[64:128, h, kc * 128:(kc + 1) * 128],
                                        rhs=qT[64:128, h, :], start=True, stop=True)
                    S.op("pe", qk, r=[("KT", h), ("R", h)], w=[psk(sb0), psk(sb0 + 1)])

                emit_qk(*steps[0])
                for si, (h, kc) in enumerate(steps):
                    if si + 1 < len(steps):
                        emit_qk(*steps[si + 1])
                    sb0 = sb_of[(h, kc)]
                    esl = kc % 2
                    if MERGE_EXP:
                        S.op("act", lambda e, sb0=sb0, esl=esl: e.activation(out=Ebuf[:, esl, :, :], in_=ps[:, sb0:sb0 + 2, :],
                                                                             func=AF.Exp, scale=0.125),
                             r=[psk(sb0), psk(sb0 + 1)], w=[("R", 16 + 2 * esl), ("R", 17 + 2 * esl)])
                    else:
                        for c in range(2):
                            S.op("act", lambda e, sb0=sb0, esl=esl, c=c: e.activation(out=Ebuf[:, esl, c, :], in_=ps[:, sb0 + c, :],
                                                                                      func=AF.Exp, scale=0.125),
                                 r=[psk(sb0 + c)], w=[("R", 16 + 2 * esl + c)])

                    def pv(e, h=h, kc=kc, esl=esl):
                        last = None
                        for qs in range(4):
                            for c in range(2):
                                a = qs * 2 + c
                                bank, off = 4 + a // 3, (a % 3) * 130
                                last = e.matmul(ps[:, bank, off:off + 130], lhsT=Ebuf[:, esl, c, qs * 128:(qs + 1) * 128],
                                                rhs=Vb[:, kc, h * 130:(h + 1) * 130], start=(kc == 0 and a % 3 == 0),
                                                stop=(kc == NKC - 1), skip_group_check=True)
                        return last
                    S.op("pe", pv, r=[("R", 16 + 2 * esl), ("R", 17 + 2 * esl), "V"], w=[psk(4), psk(5), psk(6)])
                    if h < 4 and kc == 2:
                        pool_a(i, h, uslot)
                    if h < 4 and kc == 12:
                        pool_b(h, pslot)
                    if h == 4 and kc == 0:
                        pool_end()
                    if h == 4 and kc == 6:
                        subln_pe(0, 4)
                    if kc != NKC - 1:
                        continue
                    S.op("dve", lambda e: e.tensor_copy(out=Oev[:, 0:3, :], in_=ps[:, 4, 0:390].rearrange("p (a e) -> p a e", a=3)),
                         r=[psk(4)], w=["Oev"])
                    S.op("dve", lambda e: e.tensor_copy(out=Oev[:, 3:6, :], in_=ps[:, 5, 0:390].rearrange("p (a e) -> p a e", a=3)),
                         r=[psk(5), "Oev"], w=["Oev"])
                    S.op("dve", lambda e: e.tensor_copy(out=Oev[:, 6:8, :], in_=ps[:, 6, 0:260].rearrange("p (a e) -> p a e", a=2)),
                         r=[psk(6), "Oev"], w=["Oev"])
                    S.op("dve", lambda e: e.reciprocal(out=rsO[:, :], in_=Oev[:, :, 128]), r=["Oev"], w=["rsO"])
                    S.op("dve", lambda e: e.tensor_tensor(out=Oev[:, :, 0:128], in0=Oev[:, :, 0:128],
                                                          in1=rsO[:, :].unsqueeze(2).broadcast_to([128, 8, 128]), op=ALU.mult),
                         r=["Oev", "rsO"], w=["Oev"])
                    Ov = Oev[:, :, :].rearrange("p (q c) e -> p q c e", c=2)
                    S.op("dve", lambda e, Ov=Ov: e.scalar_tensor_tensor(out=Ov[:, :, 0, 0:128], in0=Ov[:, :, 1, 0:128], scalar=nlam,
                                                                        in1=Ov[:, :, 0, 0:128], op0=ALU.mult, op1=ALU.add),
                         r=["Oev", "lsm"], w=["Oev"])
                    s = sc.get()
                    sv = scr[:, s, 0:512].rearrange("p (q e) -> p q e", q=4)
                    S.op("dve", lambda e, Ov=Ov, sv=sv: e.tensor_tensor(out=sv, in0=Ov[:, :, 0, 0:128], in1=Ov[:, :, 0, 0:128],
                                                                        op=ALU.mult), r=["Oev"], w=[("scr", s)])
                    S.op("dve", lambda e, sv=sv, h=h: e.tensor_reduce(out=ssq[:, :, h], in_=sv, axis=AX.X, op=ALU.add),
                         r=[("scr", s)], w=["ssq"])
                    S.op("dve", lambda e, Ov=Ov, h=h: e.tensor_copy(out=attn_tok[:, :, h, :], in_=Ov[:, :, 0, 0:128]),
                         r=["Oev"], w=ATK)
                    if h == 3:
                        subln_scale(0, 4)
                subln_scale(4, 8)
                subln_pe(4, 8)

            def pool_begin():
                uslot = ws.acquire(("in", P_OFF))
                pslot = ws.acquire(("wpool",))
                return uslot, pslot

            def pool_a(i, g, uslot):
                pu = 7
                proj_fm(uslot, g * 128, TT, pu)
                S.op("dve", lambda e: e.tensor_copy(out=ug[:, 8:520], in_=ps[:, pu, :]), r=[psk(pu)], w=["ug"])
                if i > 0:
                    S.op("dve", lambda e: e.tensor_copy(out=ug[:, 0:8], in_=uhalo[:, i - 1, g, 8:16]),
                         r=["uhalo", "ug"], w=["ug"])
                else:
                    S.op("dve", lambda e: e.memset(ug[:, 0:8], 0.0), r=["ug"], w=["ug"])
                if i < 3:
                    S.op("dve", lambda e: e.tensor_copy(out=ug[:, 520:528], in_=uhalo[:, i + 1, g, 0:8]),
                         r=["uhalo", "ug"], w=["ug"])
                else:
                    S.op("dve", lambda e: e.memset(ug[:, 520:528], 0.0), r=["ug"], w=["ug"])
                w = 2 ** (g + 1)
                lo, hi, sh = 1, 528, 1
                prev = sc.get()
                S.op("dve", lambda e, s=prev: e.tensor_tensor(out=scr[:, s, 1:528], in0=ug[:, 0:527], in1=ug[:, 1:528], op=ALU.add),
                     r=["ug"], w=[("scr", prev)])
                step = 2
                while step < w:
                    s2 = sc.get()
                    nlo, nhi = lo + sh, hi - sh

                    def dbl(e, s2=s2, p=prev, nlo=nlo, nhi=nhi, sh=sh):
                        return e.tensor_tensor(out=scr[:, s2, nlo:nhi], in0=scr[:, p, nlo - sh:nhi - sh],
                                               in1=scr[:, p, nlo + sh:nhi + sh], op=ALU.add)
                    S.op("dve", dbl, r=[("scr", prev)], w=[("scr", s2)])
                    lo, hi, prev = nlo, nhi, s2
                    step *= 2
                    sh *= 2
                assert lo <= 8 and hi >= 520, (lo, hi)
                pk_ = ("pbuf", g % 2)
                pbv = pbuf[:, g % 2, :]
                S.op("dve", lambda e, p=prev: e.scalar_tensor_tensor(out=pbv, in0=scr[:, p, 8:520], scalar=1.0 / w, in1=ug[:, 8:520],
                                                                   op0=ALU.mult, op1=ALU.subtract),
                     r=[("scr", prev), "ug"], w=[pk_])
                for side, cond in ((0, i == 0), (1, i == 3)):
                    if not cond:
                        continue
                    a0 = 8 if side == 0 else 512
                    o0 = 0 if side == 0 else 504
                    s3 = sc.get()
                    S.op("dve", lambda e, p=prev, s3=s3, a0=a0, side=side: e.tensor_tensor(
                        out=scr[:, s3, 0:8], in0=scr[:, p, a0:a0 + 8], in1=edge[:, g, side * 8:side * 8 + 8], op=ALU.mult),
                        r=[("scr", prev), "consts"], w=[("scr", s3)])
                    S.op("dve", lambda e, s3=s3, a0=a0, o0=o0: e.tensor_tensor(
                        out=pbuf[:, g % 2, o0:o0 + 8], in0=scr[:, s3, 0:8], in1=ug[:, a0:a0 + 8], op=ALU.subtract),
                        r=[("scr", s3), "ug", pk_], w=[pk_])

            def pool_b(g, pslot):
                wp = ring[:, pslot, 0:512].rearrange("p (g n) -> p g n", g=4)
                pp = 7
                S.op("pe", lambda e: e.matmul(ps[:, pp, :], lhsT=wp[:, g, :], rhs=pbuf[:, g % 2, :], start=True, stop=True),
                     r=[("ring", pslot), ("pbuf", g % 2)], w=[psk(pp)])
                S.op("dve", lambda e: e.tensor_scalar_mul(out=poolT[:, g, :], in0=ps[:, pp, :], scalar1=vecT[:, 104 + g:105 + g]),
                     r=[psk(pp), "vecT"], w=["rope"])

            def pool_end():
                ws.release()
                ws.release()

            def merge(xs, j, stats_after=False):
                attnT_all = [("attnT", h) for h in range(8)]
                poolT_all = ["rope"]
                for pr in range(4):
                    gslot = ws.acquire(("G", pr))
                    bslot = ws.acquire(("B", pr))
                    g8 = sl8(gslot)
                    ba8 = ring[:, bslot, 0:2048].rearrange("p (kc n) -> p kc n", kc=8)
                    bp4 = ring[:, bslot, 2048:3072].rearrange("p (kc n) -> p kc n", kc=4)
                    for dd in range(2):
                        d = pr * 2 + dd
                        pga, pgp, pza, pzp = pr6.get(), pr6.get(), pr6.get(), pr6.get()

                        def mm8(e, w, col, rhs, pk, nk=8):
                            last = None
                            for kc in range(nk):
                                last = e.matmul(ps[:, pk, :], lhsT=w[:, kc, col:col + 128], rhs=rhs[:, kc, :],
                                                start=(kc == 0), stop=(kc == nk - 1))
                            return last
                        S.op("pe", lambda e, dd=dd, pga=pga, g8=g8: mm8(e, g8, dd * 128, hT, pga),
                             r=[("ring", gslot)] + hT_all, w=[psk(pga)])
                        S.op("pe", lambda e, dd=dd, pgp=pgp, g8=g8: mm8(e, g8, 256 + dd * 128, hT, pgp),
                             r=[("ring", gslot)] + hT_all, w=[psk(pgp)])
                        sa, sp_, s1, s2 = sc.get(), sc.get(), sc.get(), sc.get()
                        S.op("act", lambda e, pga=pga, sa=sa: e.activation(out=scr[:, sa, 0:512], in_=ps[:, pga, :], func=AF.Tanh,
                                                                          scale=0.5), r=[psk(pga)], w=[("scr", sa)])
                        S.op("act", lambda e, pgp=pgp, sp_=sp_: e.activation(out=scr[:, sp_, 0:512], in_=ps[:, pgp, :], func=AF.Tanh,
                                                                            scale=0.5), r=[psk(pgp)], w=[("scr", sp_)])
                        S.op("pe", lambda e, dd=dd, pza=pza, ba8=ba8: mm8(e, ba8, dd * 128, attnT, pza),
                             r=[("ring", bslot)] + attnT_all, w=[psk(pza)])
                        S.op("pe", lambda e, dd=dd, pzp=pzp, bp4=bp4: mm8(e, bp4, dd * 128, poolT, pzp, 4),
                             r=[("ring", bslot)] + poolT_all, w=[psk(pzp)])
                        S.op("dve", lambda e, sa=sa, pza=pza, s1=s1: e.scalar_tensor_tensor(
                            out=scr[:, s1, 0:512], in0=scr[:, sa, 0:512], scalar=1.0, in1=ps[:, pza, :], op0=ALU.add, op1=ALU.mult),
                            r=[("scr", sa), psk(pza)], w=[("scr", s1)])
                        S.op("dve", lambda e, sp_=sp_, pzp=pzp, s2=s2: e.scalar_tensor_tensor(
                            out=scr[:, s2, 0:512], in0=scr[:, sp_, 0:512], scalar=1.0, in1=ps[:, pzp, :], op0=ALU.add, op1=ALU.mult),
                            r=[("scr", sp_), psk(pzp)], w=[("scr", s2)])
                        S.op("dve", lambda e, d=d, s1=s1, s2=s2: e.tensor_tensor(out=yT[:, d, :], in0=scr[:, s1, 0:512],
                                                                                in1=scr[:, s2, 0:512], op=ALU.add),
                             r=[("scr", s1), ("scr", s2)], w=[("R", d)])
                    ws.release()
                    ws.release()
                yT_all = [("R", d) for d in range(8)]
                for half in range(2):
                    slot = ws.acquire(("out", half))
                    w8 = sl8(slot)
                    for dd in range(4):
                        d = half * 4 + dd
                        po = pr6.get()

                        def mmo(e, w8=w8, dd=dd, po=po):
                            last = None
                            for kc in range(8):
                                last = e.matmul(ps[:, po, :], lhsT=w8[:, kc, dd * 128:(dd + 1) * 128], rhs=yT[:, kc, :],
                                                start=(kc == 0), stop=(kc == 7))
                            return last
                        S.op("pe", mmo, r=[("ring", slot)] + yT_all, w=[psk(po)])
                        if stats_after and d > 0:
                            stat_mm(TT, d - 1, pend)
                        S.op("dve", lambda e, d=d, po=po: e.scalar_tensor_tensor(out=xT[:, xs, d, :], in0=ps[:, po, :],
                                                                                scalar=GG[:, 1, d, j:j + 1], in1=xT[:, xs, d, :],
                                                                                op0=ALU.mult, op1=ALU.add),
                             r=[psk(po), "GG", xk(xs, d)], w=[xk(xs, d)])
                        if stats_after:
                            pend = stat_sq(xs, TT, d, "act")
                    ws.release()
                if stats_after:
                    stat_mm(TT, 7, pend)
                    stat_fin(TT)

            def prefetch_x(i_next):
                xs_n = tilec[0] % 2
                S.op("sp", lambda e: e.dma_start(out=xT[:, xs_n, :, :].rearrange("p c t -> p (c t)"), in_=xsp_d[i_next]),
                     r=[("xsp", i_next)], w=[xk(xs_n, c) for c in range(8)], dma=("xload", xs_n))

            def phase_b_tile(b, i):
                xs = tilec[0] % 2
                tilec[0] += 1
                j = b
                if i < 3:
                    prefetch_x(i + 1)
                if stage >= 3.2:
                    S.op("sp", lambda e: [e.dma_start(out=rope[:, 0, :], in_=ropec_d[:, i * TT:(i + 1) * TT]),
                                          e.dma_start(out=rope[:, 1, :], in_=ropes_d[:, i * TT:(i + 1) * TT])],
                         w=["rope"], dma="rope", ninst=2)
                    norm_mod(xs, TT, 1, j)
                    for half in range(2):
                        slot = ws.acquire(("in", half * 512))
                        for hh in range(4):
                            h = half * 4 + hh
                            pk = pr6.get()
                            proj_fm(slot, hh * 128, TT, pk, split=(h == 0))
                            rope_to(pk, TT, qT[:, h, :], [("R", h)])
                        ws.release()
                    if stage >= 3.4:
                        attention(xs, i)
                    if stage >= 3.5:
                        merge(xs, j, stats_after=(stage >= 4))
                if stage >= 4:
                    ffn(xs, TT, 1, j, 2, have_stats=True, stats_after=True)
                    for c in range(8):
                        s = sc.get()
                        S.op("dve", lambda e, c=c, s=s: e.tensor_tensor(out=scr[:, s, 0:TT], in0=xT[:, xs, c, :], in1=rstd[:, :],
                                                                        op=ALU.mult), r=[xk(xs, c), "rstd"], w=[("scr", s)])
                        S.op("act", lambda e, c=c, s=s: e.activation(out=xT[:, xs, c, :], in_=scr[:, s, 0:TT], func=AF.Identity,
                                                                     scale=vecT[:, 96 + c:97 + c]),
                             r=[("scr", s), "vecT"], w=[xk(xs, c)])
                store_tile(b, i * TT, xs)

            for b in range(NB):
                phase_a_tile(b, True, 0)
                if stage >= 2:
                    phase_a_tile(b, False, 0)
                for i in range(1, 4):
                    phase_a_tile(b, True, i)
                if b == 0 and stage >= 2:
                    mods_part(12, 18, 2, 3)
                for i in range(4):
                    phase_b_tile(b, i)

        S0 = Sched()
        ws0 = WStream(S0, fill_fn)
        program(S0, ws0)
        tags = ws0.tags
        img_tags.clear()
        S = Sched()
        ws = WStream(S, fill_fn, tags)
        program(S, ws)
        assert ws.na == len(tags) and ws.nf == len(tags), (ws.na, ws.nf, len(tags))
        cnt = S.analyse()
        sems = {}
        for key in cnt:
            nm = "s_" + "_".join(str(x) for x in (key[1] if isinstance(key[1], tuple) else (key[1],)))
            sems[key] = es.enter_context(nc.semaphore(nm))
        S.emit(nc, {"pe": None, "act": None, "dve": None, "pool": None, "sp": None}, sems)
        with nc.Block() as block:
            @block.tensor
            def _(e):
                S.emit_engine("pe", e, sems)

            @block.scalar
            def _(e):
                S.emit_engine("act", e, sems)

            @block.vector
            def _(e):
                S.emit_engine("dve", e, sems)

            @block.gpsimd
            def _(e):
                S.emit_engine("pool", e, sems)

            @block.sync
            def _(e):
                S.emit_engine("sp", e, sems)
    return nc


def _consts():
    half = 16
    freqs = (10000.0 ** (-np.arange(half, dtype=np.float32) / half)).astype(np.float32)
    t = np.arange(L)
    row = (t // 64).astype(np.float32)
    col = (t % 64).astype(np.float32)
    ropec = np.zeros((128, L), np.float32)
    ropes = np.zeros((128, L), np.float32)
    perm = np.zeros((128, 128), np.float32)
    for p in range(128):
        d = p % 64
        axis = d // 32
        jj = d % 32
        pos = row if axis == 0 else col
        ang = (pos * freqs[jj % 16]).astype(np.float32)
        ropec[p] = np.cos(ang)
        if jj < 16:
            ropes[p] = -np.sin(ang)
            partner = p + 16
        else:
            ropes[p] = np.sin(ang)
            partner = p - 16
        perm[partner, p] = 1.0
    edge = np.zeros((4, 16), np.float32)
    for g, w in enumerate((2, 4, 8, 16)):
        for tt in range(8):
            lo = max(tt - w // 2, 0)
            hi = min(tt + w - w // 2, L)
            edge[g, tt] = 1.0 / (hi - lo)
            t2 = L - 8 + tt
            lo = max(t2 - w // 2, 0)
            hi = min(t2 + w - w // 2, L)
            edge[g, 8 + tt] = 1.0 / (hi - lo)
    edge_bc = np.ascontiguousarray(np.broadcast_to(edge.reshape(1, 64), (128, 64)))
    return ropec, ropes, perm, np.eye(128, dtype=np.float32), edge_bc


def make_in_maps(inp, ncores=NCORES):
    f = lambda a: np.ascontiguousarray(np.asarray(a, dtype=np.float32))
    ropec, ropes, perm, ident, edge_bc = _consts()
    vec1 = np.concatenate([f(inp["b_mod"]).reshape(72, 128), f(inp["g_norm"]).reshape(24, 128),
                           f(inp["g_final"]).reshape(8, 128), f(inp["pool_scale"]).reshape(4, 128)], axis=0)
    lam = np.concatenate([f(inp["lambda_q1"]).reshape(1, 64), f(inp["lambda_k1"]).reshape(1, 64),
                          f(inp["lambda_q2"]).reshape(1, 64), f(inp["lambda_k2"]).reshape(1, 64)], axis=1)
    lam_bc = np.ascontiguousarray(np.broadcast_to(lam, (128, 256)))
    gsub_bc = np.ascontiguousarray(np.broadcast_to(f(inp["g_subln"]).reshape(1, 128), (128, 128)))
    shared = {
        "vec1": np.ascontiguousarray(vec1), "w_mod": f(inp["w_mod"])[0], "w_gu": f(inp["w_ffn_gu"])[0],
        "w_down": f(inp["w_ffn_down"])[0], "w_in": f(inp["w_in"])[0], "w_ba": f(inp["w_branch_attn"])[0],
        "w_bp": f(inp["w_branch_pool"])[0], "w_out": f(inp["w_out"])[0], "w_pool": f(inp["w_pool"])[0],
        "lam_bc": lam_bc, "gsub_bc": gsub_bc, "ropec": ropec, "ropes": ropes, "ident": ident, "perm": perm,
        "edge": edge_bc,
    }
    x = f(inp["x"])
    ctx = f(inp["ctx"])
    c = f(inp["c"])
    c_ctx = f(inp["c_ctx"])
    maps = []
    for k in range(ncores):
        cc = np.concatenate([c[NB * k:NB * k + NB], c_ctx.reshape(1, D)], axis=0).reshape(24, 128)
        m = dict(shared)
        m["x"] = np.ascontiguousarray(x[NB * k:NB * k + NB])
        m["ctx"] = np.ascontiguousarray(ctx[NB * k:NB * k + NB])
        m["cc"] = np.ascontiguousarray(cc)
        maps.append(m)
    return maps


_NC_CACHE = {}


def run(inp, stage=4, ncores=NCORES, trace=False):
    if stage not in _NC_CACHE:
        _NC_CACHE[stage] = build_nc(stage)
    nc = _NC_CACHE[stage]
    maps = make_in_maps(inp, ncores)
    res = run_bass_kernel_spmd(nc, maps, core_ids=list(range(ncores)), trace=trace)
    out = np.concatenate([r["out"] for r in res.results], axis=0)
    return out, res


def kernel(**inputs):
    out, _ = run(inputs, stage=4, ncores=NCORES)
    return out.astype(np.float32)
```
